# Optimizing a Trainium2 kernel written in Bass

```python
import math
import jax
import jax.numpy as jnp
from jax import lax
import numpy as np

D_MODEL = 1024
BATCH = 8
SEQ = 2048
DEPTH = 1

N_META = 16
D_MIX = 2 * D_MODEL
SSD_WIDTH = D_MIX // 2
SSD_HEAD_DIM = 64
SSD_HEADS = SSD_WIDTH // SSD_HEAD_DIM
SSD_GROUPS = 2
SSD_HPG = SSD_HEADS // SSD_GROUPS
SSD_STATE = 128
SSD_CONV = 4
SSD_CHUNK = 128
SSD_CONV_DIM = SSD_WIDTH + 2 * SSD_GROUPS * SSD_STATE
LRU_WIDTH = D_MIX - SSD_WIDTH
LRU_BLOCKS = 16
LRU_BLOCK_W = LRU_WIDTH // LRU_BLOCKS
LRU_CONV = 4
LRU_C = 8.0
D_FF = -(-(8 * D_MODEL) // (3 * 256)) * 256
IN_COLS = SSD_WIDTH + SSD_CONV_DIM + SSD_HEADS + 2 * LRU_WIDTH
IN_SPLITS = [SSD_WIDTH, SSD_WIDTH + SSD_CONV_DIM, SSD_WIDTH + SSD_CONV_DIM + SSD_HEADS, SSD_WIDTH + SSD_CONV_DIM + SSD_HEADS + LRU_WIDTH]
EPS = 1e-6

kernel_name = 'hymba_ssd_rglru_hybrid_block'


def rmsnorm(x, w):
    xf = x.astype(jnp.float32)
    y = xf * lax.rsqrt(jnp.mean(xf * xf, axis=-1, keepdims=True) + EPS)
    return (y * w.astype(jnp.float32)).astype(x.dtype)


def causal_dwconv(x, w, b):
    k, c = w.shape
    y = lax.conv_general_dilated(x, w[:, None, :].astype(x.dtype), window_strides=(1,), padding=[(k - 1, 0)], dimension_numbers=('NWC', 'WIO', 'NWC'), feature_group_count=c)
    return y + b.astype(x.dtype)


def _to_chunks(t, pad):
    t = jnp.pad(t, [(0, 0), (pad, 0)] + [(0, 0)] * (t.ndim - 2))
    return t.reshape((t.shape[0], -1, SSD_CHUNK) + t.shape[2:])


def ssd_mixer(z, xbc, dt_raw, conv_w, conv_b, dt_bias, a_log, d_skip, norm_w):
    bsz, seqlen, _ = z.shape
    f32 = jnp.float32
    xbc = jax.nn.silu(causal_dwconv(xbc, conv_w, conv_b))
    xs, b_in, c_in = jnp.split(xbc, [SSD_WIDTH, SSD_WIDTH + SSD_GROUPS * SSD_STATE], axis=-1)
    dt = jax.nn.softplus(dt_raw.astype(f32) + dt_bias.astype(f32))
    a = -jnp.exp(a_log.astype(f32)).reshape(SSD_GROUPS, SSD_HPG)
    pad = (-seqlen) % SSD_CHUNK
    x_c = _to_chunks(xs.astype(f32).reshape(bsz, seqlen, SSD_GROUPS, SSD_HPG, SSD_HEAD_DIM), pad)
    b_c = _to_chunks(b_in.astype(f32).reshape(bsz, seqlen, SSD_GROUPS, SSD_STATE), pad)
    c_c = _to_chunks(c_in.astype(f32).reshape(bsz, seqlen, SSD_GROUPS, SSD_STATE), pad)
    dt_c = _to_chunks(dt.reshape(bsz, seqlen, SSD_GROUPS, SSD_HPG), pad)
    cs = jnp.cumsum(dt_c * a, axis=2)
    xdt = x_c * dt_c[..., None]
    causal = jnp.tril(jnp.ones((SSD_CHUNK, SSD_CHUNK), dtype=bool))
    seg = cs[:, :, :, None] - cs[:, :, None, :]
    lmat = jnp.exp(jnp.where(causal[:, :, None, None], seg, -jnp.inf))
    cb = jnp.einsum('bclgn,bcsgn->bclsg', c_c, b_c)
    y_diag = jnp.einsum('bclsgj,bcsgjp->bclgjp', cb[..., None] * lmat, xdt)
    decay_states = jnp.exp(cs[:, :, -1:] - cs)
    states = jnp.einsum('bclgn,bclgjp->bcgjpn', b_c, xdt * decay_states[..., None])
    chunk_decay = jnp.exp(cs[:, :, -1])

    def step(h, inp):
        s, d = inp
        return h * d[..., None, None] + s, h

    h0 = jnp.zeros((bsz, SSD_GROUPS, SSD_HPG, SSD_HEAD_DIM, SSD_STATE), f32)
    _, prev = lax.scan(step, h0, (jnp.moveaxis(states, 1, 0), jnp.moveaxis(chunk_decay, 1, 0)))
    prev = jnp.moveaxis(prev, 0, 1)
    y_off = jnp.einsum('bclgn,bcgjpn->bclgjp', c_c, prev) * jnp.exp(cs)[..., None]
    y = (y_diag + y_off).reshape(bsz, -1, SSD_WIDTH)[:, pad:]
    y = y + (xs.astype(f32).reshape(bsz, seqlen, SSD_HEADS, SSD_HEAD_DIM) * d_skip.astype(f32)[:, None]).reshape(bsz, seqlen, SSD_WIDTH)
    y = y.astype(z.dtype)
    g = (y * jax.nn.silu(z)).reshape(bsz, seqlen, SSD_GROUPS, SSD_WIDTH // SSD_GROUPS)
    return rmsnorm(g, norm_w.reshape(SSD_GROUPS, -1)).reshape(bsz, seqlen, SSD_WIDTH)


def rglru_mixer(gate, xr, conv_w, conv_b, wa, ba, wx, bx, lam, norm_w):
    bsz, seqlen, _ = xr.shape
    f32 = jnp.float32
    xr = causal_dwconv(xr, conv_w, conv_b)
    xb = xr.reshape(bsz, seqlen, LRU_BLOCKS, LRU_BLOCK_W)
    r = jax.nn.sigmoid(jnp.einsum('btni,nij->btnj', xb, wa).reshape(bsz, seqlen, LRU_WIDTH) + ba)
    i = jax.nn.sigmoid(jnp.einsum('btni,nij->btnj', xb, wx).reshape(bsz, seqlen, LRU_WIDTH) + bx)
    log_a = -LRU_C * r.astype(f32) * jax.nn.softplus(-lam.astype(f32))
    a = jnp.exp(log_a)
    u = jnp.sqrt(-jnp.expm1(2.0 * log_a)) * (i * xr).astype(f32)

    def combine(left, right):
        a1, b1 = left
        a2, b2 = right
        return a1 * a2, a2 * b1 + b2

    _, h = lax.associative_scan(combine, (a, u), axis=1)
    y = jax.nn.gelu(gate) * h.astype(gate.dtype)
    return rmsnorm(y, norm_w)


def setup_inputs(seed: int = 0) -> dict:
    key = jax.random.key(seed)
    ks = jax.random.split(key, 24)
    nrm = jax.random.normal
    dt0 = jnp.exp(jax.random.uniform(ks[6], (DEPTH, SSD_HEADS), minval=math.log(1e-3), maxval=math.log(1e-1)))
    a_base = jax.random.uniform(ks[15], (DEPTH, LRU_WIDTH), minval=0.9, maxval=0.999)
    s = a_base ** (1.0 / LRU_C)
    return {
        'x': nrm(ks[0], (BATCH, SEQ, D_MODEL), jnp.float32),
        'meta_tokens': nrm(ks[1], (N_META, D_MODEL), jnp.float32),
        'norm1_w': 1.0 + 0.02 * nrm(ks[2], (DEPTH, D_MODEL)),
        'w_in': nrm(ks[3], (DEPTH, D_MODEL, IN_COLS)) * D_MODEL ** -0.5,
        'ssd_conv_w': nrm(ks[4], (DEPTH, SSD_CONV, SSD_CONV_DIM)) * SSD_CONV ** -0.5,
        'ssd_conv_b': 0.02 * nrm(ks[5], (DEPTH, SSD_CONV_DIM)),
        'ssd_dt_bias': dt0 + jnp.log(-jnp.expm1(-dt0)),
        'ssd_a_log': jnp.log(jax.random.uniform(ks[7], (DEPTH, SSD_HEADS), minval=1.0, maxval=16.0)),
        'ssd_d': 1.0 + 0.1 * nrm(ks[8], (DEPTH, SSD_HEADS)),
        'ssd_norm_w': 1.0 + 0.02 * nrm(ks[9], (DEPTH, SSD_WIDTH)),
        'lru_conv_w': nrm(ks[10], (DEPTH, LRU_CONV, LRU_WIDTH)) * LRU_CONV ** -0.5,
        'lru_conv_b': 0.02 * nrm(ks[11], (DEPTH, LRU_WIDTH)),
        'lru_wa': nrm(ks[12], (DEPTH, LRU_BLOCKS, LRU_BLOCK_W, LRU_BLOCK_W)) * LRU_BLOCK_W ** -0.5,
        'lru_ba': 0.02 * nrm(ks[13], (DEPTH, LRU_WIDTH)),
        'lru_wx': nrm(ks[14], (DEPTH, LRU_BLOCKS, LRU_BLOCK_W, LRU_BLOCK_W)) * LRU_BLOCK_W ** -0.5,
        'lru_bx': 0.02 * nrm(ks[16], (DEPTH, LRU_WIDTH)),
        'lru_lambda': jnp.log(s) - jnp.log1p(-s),
        'lru_norm_w': 1.0 + 0.02 * nrm(ks[17], (DEPTH, LRU_WIDTH)),
        'w_out': nrm(ks[18], (DEPTH, D_MIX, D_MODEL)) * D_MIX ** -0.5,
        'norm2_w': 1.0 + 0.02 * nrm(ks[19], (DEPTH, D_MODEL)),
        'w_gate': nrm(ks[20], (DEPTH, D_MODEL, D_FF)) * D_MODEL ** -0.5,
        'w_up': nrm(ks[21], (DEPTH, D_MODEL, D_FF)) * D_MODEL ** -0.5,
        'w_down': nrm(ks[22], (DEPTH, D_FF, D_MODEL)) * D_FF ** -0.5,
        'final_norm_w': 1.0 + 0.02 * nrm(ks[23], (D_MODEL,)),
    }


def reference(x, meta_tokens, norm1_w, w_in, ssd_conv_w, ssd_conv_b, ssd_dt_bias, ssd_a_log, ssd_d, ssd_norm_w, lru_conv_w, lru_conv_b, lru_wa, lru_ba, lru_wx, lru_bx, lru_lambda, lru_norm_w, w_out, norm2_w, w_gate, w_up, w_down, final_norm_w):
    bsz = x.shape[0]
    meta = jnp.broadcast_to(meta_tokens.astype(x.dtype)[None], (bsz, N_META, D_MODEL))
    h = jnp.concatenate([meta, x], axis=1)
    for li in range(DEPTH):
        u = rmsnorm(h, norm1_w[li])
        proj = u @ w_in[li]
        z, xbc, dt_raw, g_lru, x_lru = jnp.split(proj, IN_SPLITS, axis=-1)
        y_ssd = ssd_mixer(z, xbc, dt_raw, ssd_conv_w[li], ssd_conv_b[li], ssd_dt_bias[li], ssd_a_log[li], ssd_d[li], ssd_norm_w[li])
        y_lru = rglru_mixer(g_lru, x_lru, lru_conv_w[li], lru_conv_b[li], lru_wa[li], lru_ba[li], lru_wx[li], lru_bx[li], lru_lambda[li], lru_norm_w[li])
        h = h + jnp.concatenate([y_ssd, y_lru], axis=-1) @ w_out[li]
        u = rmsnorm(h, norm2_w[li])
        h = h + (jax.nn.silu(u @ w_gate[li]) * (u @ w_up[li])) @ w_down[li]
    h = rmsnorm(h, final_norm_w)
    return h[:, N_META:]
```

```python
import numpy as np
from contextlib import ExitStack
import concourse.bass as bass
import concourse.mybir as mybir
from concourse.bass_utils import run_bass_kernel_spmd

F32 = mybir.dt.float32
BF16 = mybir.dt.bfloat16
AF = mybir.ActivationFunctionType
ALU = mybir.AluOpType

ENGS = ("pe", "dve", "act", "pool", "sp")
CENG = ("pe", "dve", "act", "pool")

D = 1024
SEQ = 2048
NMETA = 16
T = SEQ + NMETA
DFF = 2816
NFC = DFF // 128
EPS = 1e-6
C_Z, C_X, C_B, C_C, C_DT, C_LG, C_LX = 0, 1024, 2048, 2304, 2560, 2576, 3600
GRP = [(0, 16)] + [(16 + 512 * j, 512) for j in range(4)]


def toff(i):
    return 0 if i == 0 else 16 + 128 * (i - 1)


def tn(i):
    return 16 if i == 0 else 128


class Sched:
    def __init__(self):
        self.streams = {e: [] for e in ENGS}
        self.nops = {e: 0 for e in ENGS}
        self.known = {e: {} for e in ENGS}
        self.clock = {}
        self.reg = {}
        self.bank = {}
        self.dma_cnt = {}
        self.targets = set()

    def _deps(self, eng, reads, writes, banks):
        deps = []
        for r in reads:
            st = self.reg.get(r)
            if st and st["w"]:
                deps.append((st["w"], "raw"))
        for w in writes:
            st = self.reg.get(w)
            if st:
                if st["w"]:
                    deps.append((st["w"], "waw"))
                for sk, idx in st["r"].items():
                    deps.append(((sk, idx), "war"))
        for b in banks:
            for sk, idx in self.bank.get(b, {}).items():
                if sk != eng:
                    deps.append(((sk, idx), "bank"))
        need = {}
        for (sk, idx), kind in deps:
            if sk == eng:
                if eng in ("pe", "sp"):
                    continue
            if need.get(sk, 0) < idx:
                need[sk] = idx
        return need

    def _apply_waits(self, eng, need):
        kn = self.known[eng]
        waits = [(sk, idx) for sk, idx in need.items() if kn.get(sk, 0) < idx]
        for sk, idx in waits:
            if kn.get(sk, 0) < idx:
                kn[sk] = idx
            ck = self.clock.get((sk, idx))
            if ck:
                for k2, v2 in ck.items():
                    if k2 != eng and kn.get(k2, 0) < v2:
                        kn[k2] = v2
            if not sk.startswith("d:"):
                self.targets.add((sk, idx))
        return waits

    def op(self, eng, fn, reads=(), writes=(), banks=()):
        need = self._deps(eng, reads, writes, banks)
        waits = self._apply_waits(eng, need)
        self.nops[eng] += 1
        idx = self.nops[eng]
        self.clock[(eng, idx)] = dict(self.known[eng])
        self.streams[eng].append([waits, fn, "op", (eng, idx)])
        for r in reads:
            st = self.reg.setdefault(r, {"w": None, "r": {}})
            st["r"][eng] = idx
        for w in writes:
            self.reg[w] = {"w": (eng, idx), "r": {}}
        for b in banks:
            self.bank.setdefault(b, {})[eng] = idx
        return (eng, idx)

    def dma(self, queue, fn, sem, reads=(), writes=()):
        sk = "d:" + sem
        need = self._deps(queue, reads, writes, ())
        waits = self._apply_waits(queue, need)
        self.dma_cnt[sk] = self.dma_cnt.get(sk, 0) + 1
        idx = self.dma_cnt[sk]
        self.clock[(sk, idx)] = dict(self.known[queue])
        self.streams[queue].append([waits, fn, "dma", (sk, idx)])
        for r in reads:
            st = self.reg.setdefault(r, {"w": None, "r": {}})
            st["r"][sk] = idx
        for w in writes:
            self.reg[w] = {"w": (sk, idx), "r": {}}
        return (sk, idx)

    def barrier(self):
        items = [(e, self.nops[e]) for e in CENG if self.nops[e] > 0]
        items += [(sk, c) for sk, c in self.dma_cnt.items()]
        for e in ENGS:
            need = {}
            for sk, idx in items:
                need[sk] = idx
            waits = self._apply_waits(e, need)
            self.streams[e].append([waits, None, "wait", None])
        self.reg = {}
        self.bank = {}

    def final_wait(self, eng, items):
        need = {}
        for sk, idx in items:
            if need.get(sk, 0) < idx:
                need[sk] = idx
        waits = self._apply_waits(eng, need)
        self.streams[eng].append([waits, None, "wait", None])

    def emit(self, nc, stack):
        sems = {}
        for e in CENG:
            sems[e] = stack.enter_context(nc.semaphore("s_" + e))
        for sk in self.dma_cnt:
            sems[sk] = stack.enter_context(nc.semaphore("s_" + sk.replace(":", "_")))
        rank = {}
        for e in CENG:
            ids = sorted(i for (s, i) in self.targets if s == e)
            rank[e] = {i: n + 1 for n, i in enumerate(ids)}

        def run(eng_name, eng):
            for waits, fn, kind, key in self.streams[eng_name]:
                for sk, idx in waits:
                    val = 16 * idx if sk.startswith("d:") else rank[sk][idx]
                    eng.wait_ge(sems[sk], val)
                if fn is None:
                    continue
                ins = fn(eng)
                if kind == "dma":
                    ins.then_inc(sems[key[0]], 16)
                elif key in self.targets:
                    ins.then_inc(sems[key[0]], 1)

        block = stack.enter_context(nc.Block())

        @block.tensor
        def _(e):
            run("pe", e)

        @block.vector
        def _(e):
            run("dve", e)

        @block.scalar
        def _(e):
            run("act", e)

        @block.gpsimd
        def _(e):
            run("pool", e)

        @block.sync
        def _(e):
            run("sp", e)


def build(debug=None):
    debug = debug or {}
    stop_after = debug.get("stop_after")
    nc = bass.Bass("TRN2", target_bir_lowering=False)

    def din(name, shape):
        return nc.dram_tensor(name, list(shape), F32, kind="ExternalInput").ap()

    x = din("x", [SEQ, D])
    meta = din("meta", [NMETA, D])
    norm1_w = din("norm1_w", [1, D])
    w_in = din("w_in", [D, 4624])
    cw_s = din("cw_s", [128, 12, 4])
    cb_s_col = din("cb_s_col", [128, 12])
    cb_s_row = din("cb_s_row", [1, 1536])
    dt_bias = din("dt_bias", [1, 16])
    a_log = din("a_log", [1, 16])
    d_skip = din("d_skip", [1, 16])
    ssd_nw = din("ssd_nw", [1, D])
    lvec = din("lvec", [128, 8, 9])
    lru_wa = din("lru_wa", [16, 64, 64])
    lru_wx = din("lru_wx", [16, 64, 64])
    w_out = din("w_out", [2 * D, D])
    norm2_w = din("norm2_w", [1, D])
    w_gate = din("w_gate", [D, DFF])
    w_up = din("w_up", [D, DFF])
    w_down = din("w_down", [DFF, D])
    final_nw = din("final_nw", [1, D])
    out = nc.dram_tensor("out", [SEQ, D], F32, kind="ExternalOutput").ap()
    dbg = {}
    for name, shape in debug.get("dumps", {}).items():
        dbg[name] = nc.dram_tensor("dbg_" + name, list(shape), F32, kind="ExternalOutput").ap()

    S = Sched()

    def OP(eng, meth, reads=(), writes=(), banks=(), **kw):
        S.op(eng, lambda e: getattr(e, meth)(**kw), reads, writes, banks)

    def DMA(queue, sem, out, in_, reads=(), writes=()):
        S.dma(queue, lambda e: e.dma_start(out=out, in_=in_), sem, reads, writes)

    def MM(out, lhsT, rhs, start, stop, reads, bank):
        S.op("pe", lambda e: e.matmul(out, lhsT=lhsT, rhs=rhs, start=start, stop=stop), reads, [("P", bank)], [bank])

    with ExitStack() as st:
        def sb(name, shape, dt):
            return st.enter_context(nc.sbuf_tensor(name, list(shape), dt))

        ARENA_KB = 202
        arena = sb("arena", [128, ARENA_KB * 512], BF16)

        def carve(off_b, shape, dt):
            nel = int(np.prod(shape[1:]))
            nbytes = nel * (4 if dt == F32 else 2)
            assert off_b % 4 == 0 and off_b + nbytes <= ARENA_KB * 1024, (off_b, nbytes)
            v = arena[0:shape[0], off_b // 2: off_b // 2 + nbytes // 2]
            if dt != BF16:
                v = v.bitcast(dt)
            if len(shape) == 3:
                v = v.rearrange("p (a b) -> p a b", a=shape[1])
            elif len(shape) == 4:
                v = v.rearrange("p (a b c) -> p a b c", a=shape[1], b=shape[2])
            return v

        KB = 1024
        P = [st.enter_context(nc.psum_tensor("P%d" % i, [128, 512], F32)) for i in range(8)]

        ident_bf = sb("ident_bf", [128, 128], BF16)
        ones_bf = sb("ones_bf", [128, 128], BF16)
        ones_f = sb("ones_f", [128, 128], F32)
        zero_f = sb("zero_f", [128, 128], F32)
        tri_f = sb("tri_f", [128, 128], F32)
        mneg_bf = sb("mneg_bf", [128, 128], BF16)
        stat = sb("stat", [128, 8, 20], F32)
        rstd_l = sb("rstd_l", [128, 16], F32)
        junk = sb("junk", [128, 1024], BF16)

        def dbg_dump(name, ap_sb, reads=()):
            if name in dbg:
                DMA("sp", "dbg", dbg[name], ap_sb, reads=list(reads))

        def finish():
            items = [(sk, c) for sk, c in S.dma_cnt.items() if sk in ("d:dbg", "d:out0", "d:out1")]
            S.final_wait("sp", items)
            S.barrier()
            S.emit(nc, st)

        OP("pool", "memset", writes=["ones_bf"], ap=ones_bf[:], constant=1.0)
        OP("pool", "memset", writes=["ones_f"], ap=ones_f[:], constant=1.0)
        OP("pool", "memset", writes=["zero_f"], ap=zero_f[:], constant=0.0)
        OP("pool", "affine_select", reads=["ones_bf"], writes=["ident_bf"], out=ident_bf[:], in_=ones_bf[:], pattern=[[1, 128]],
           compare_op=ALU.is_equal, fill=0.0, base=0, channel_multiplier=-1)
        OP("pool", "affine_select", reads=["ones_f"], writes=["tri_f"], out=tri_f[:], in_=ones_f[:], pattern=[[1, 128]],
           compare_op=ALU.is_ge, fill=0.0, base=0, channel_multiplier=-1)
        OP("pool", "affine_select", reads=["zero_f"], writes=["mneg_bf"], out=mneg_bf[:], in_=zero_f[:], pattern=[[1, 128]],
           compare_op=ALU.is_ge, fill=-30000.0, base=0, channel_multiplier=-1)

        def ut_reads(t0, n):
            return [("uT", i) for i in range(17) if toff(i) < t0 + n and toff(i) + tn(i) > t0]

        def norm_stats(n, src, src_reads, statcol, dim=D):
            ss = stat[:n, 0, statcol:statcol + 1]
            sq = stat[:n, 1, statcol:statcol + 1]
            rs = stat[:n, 2, statcol:statcol + 1]
            OP("dve", "scalar_tensor_tensor", reads=src_reads, writes=["junk", ("st0", statcol)], out=junk[:n], in0=src, scalar=1.0, in1=src,
               op0=ALU.mult, op1=ALU.mult, accum_out=ss)
            OP("act", "activation", reads=[("st0", statcol)], writes=[("st1", statcol)], out=sq, in_=ss, func=AF.Ln, scale=1.0 / dim, bias=EPS)
            OP("act", "activation", reads=[("st1", statcol)], writes=[("st2", statcol)], out=rs, in_=sq, func=AF.Exp, scale=-0.5)

        def norm_scale(n, src, src_reads, nw_bc, nwkey, ub, ubkey, statcol):
            rs = stat[:n, 2, statcol:statcol + 1]
            OP("dve", "scalar_tensor_tensor", reads=list(src_reads) + [("st2", statcol), nwkey], writes=[ubkey], out=ub[:n], in0=src,
               scalar=rs, in1=nw_bc[:n], op0=ALU.mult, op1=ALU.mult)

        def norm_apply(n, src, src_reads, nw_bc, nwkey, ub, ubkey, bankid, dst, dst_writes, statcol):
            norm_scale(n, src, src_reads, nw_bc, nwkey, ub, ubkey, statcol)
            norm_tr(n, ub, ubkey, bankid, dst, dst_writes)

        def norm_tr(n, ub, ubkey, bankid, dst, dst_writes):
            Pb = P[bankid][:].bitcast(BF16)
            for kc in range(8):
                S.op("pe", lambda e, kc=kc: e.transpose(out=Pb[:, kc * 128: kc * 128 + n], in_=ub[:n, kc * 128:(kc + 1) * 128],
                                                        identity=ident_bf[:n, :n]),
                     [ubkey, "ident_bf"], [("P", bankid)], [bankid])
            src_ps = Pb.rearrange("p (k m) -> p k m", k=8)[:, :, 0:n]
            OP("act", "activation", reads=[("P", bankid)], writes=dst_writes, banks=[bankid], out=dst, in_=src_ps, func=AF.Copy)

        KBb = KB
        yT_ssd = carve(33 * KB, [128, 8, SEQ], BF16)
        B0 = 65 * KB
        zs = carve(B0, [128, 16, 512], BF16)
        pre = carve(B0 + 16 * KB, [128, 6, T + 3], BF16)
        BT = carve(B0 + 41 * KB, [128, T], BF16)
        CT = carve(B0 + 41 * KB + 4352, [128, T], BF16)
        WS0 = B0 + 50 * KB
        wsl = [carve(WS0 + i * 8 * KB, [128, 8, 512], BF16) for i in range(3)]
        xs_all = carve(WS0, [128, 17, 512], BF16)
        btm_all = carve(WS0 + 17408, [128, 17, 128], BF16)
        CB = WS0 + 24 * KB
        diag_s = carve(CB, [128, 12, 4, 128], BF16)
        DI = carve(CB + 12 * KB, [128, 16, 128], BF16)
        sel2 = carve(CB + 16 * KB, [80, 16, 128], BF16)
        nws_bc = carve(CB + 20 * KB, [128, 1024], F32)
        cbrow_bf = carve(CB + 24 * KB, [1, 1536], BF16)
        csT2 = carve(CB + 27 * KB, [80, 17, 128], BF16)
        ncsT2 = carve(CB + 27 * KB + 4608, [80, 17, 128], BF16)
        SM = CB + 36 * KB
        names = ["xb", "ax", "ex", "dt", "dta", "cs", "ecs", "cd", "dec", "wdec", "tmp"]
        sm = {nm: carve(SM + k * 1152, [128, 17, 16], F32) for k, nm in enumerate(names)}
        SM2 = SM + 11 * 1152
        wdt = carve(SM2, [128, 8, 16], BF16)
        cws = carve(SM2 + 256, [128, 12, 4], F32)
        cbcol = carve(SM2 + 448, [128, 12], F32)
        dtb_bc = carve(SM2 + 512, [128, 16], F32)
        alog_bc = carve(SM2 + 576, [128, 16], F32)
        a_bc = carve(SM2 + 640, [128, 16], F32)
        d_bc = carve(SM2 + 704, [128, 16], F32)
        TQ = SM2 + 1 * KB
        dta3 = carve(TQ, [128, 17, 80], F32)
        onesX = carve(TQ + 5632, [80, 2048], BF16)
        hiA = carve(TQ + 5632 + 4 * KB, [80, 4, 128], BF16)
        midA = carve(TQ + 5632 + 5 * KB, [80, 4, 128], BF16)
        loA = carve(TQ + 5632 + 6 * KB, [80, 4, 128], BF16)
        r1 = carve(53 * KB, [80, 4, 128], F32)
        r2 = carve(55 * KB, [80, 4, 128], F32)

        wcnt = [0]

        def wload(src_ap, wslots, dst_cols=None):
            s = wcnt[0] % 3
            wcnt[0] += 1
            ncol = src_ap.shape[1]
            c0 = 0 if dst_cols is None else dst_cols
            DMA("pool", "w%d" % s, wslots[s][:, :, c0:c0 + ncol], src_ap.rearrange("(kc p) n -> p kc n", p=128), writes=[("wsl", s)])
            return s

        rr = [0]

        def nbank(nb=4):
            b = rr[0] % nb
            rr[0] += 1
            return b

        LW = B0 + 16 * KB
        xw_sb = carve(LW, [128, 512], BF16)
        xdt_sb = [carve(LW + (1 + i) * KB, [128, 512], BF16) for i in range(2)]
        prev32 = carve(LW + 3 * KB, [128, 512], F32)
        prev_bf = carve(LW + 5 * KB, [128, 512], BF16)
        es_sb = [carve(LW + 6 * KB + i * 2 * KB, [128, 512], F32) for i in range(4)]
        M_sb = [carve(LW + 14 * KB + i * KB, [128, 4, 128], BF16) for i in range(4)]
        t1_sb = carve(LW + 18 * KB, [128, 512], F32)
        gn_sb = carve(LW + 20 * KB, [128, 512], BF16)
        h8 = lambda ap: ap.rearrange("p (h d) -> p h d", h=8)

        def inproj_w(g):
            OP("pool", "memset", writes=["prepad"], ap=pre[:, :, 0:3], constant=0.0)
            if g == 1:
                s_x = 3
            else:
                s_x = wload(w_in[:, C_X + g * 512: C_X + (g + 1) * 512], wsl)
            s_bc = wload(w_in[:, C_B + g * 128: C_B + (g + 1) * 128], wsl)
            DMA("pool", "w%db" % s_bc, wsl[s_bc][:, :, 128:256], w_in[:, C_C + g * 128: C_C + (g + 1) * 128].rearrange("(kc p) n -> p kc n", p=128),
                writes=[("wsl2", s_bc)])
            s_z = wload(w_in[:, C_Z + g * 512: C_Z + (g + 1) * 512], wsl)
            return s_z, s_x, s_bc

        def inproj_x_unit(slots, ct, o, n):
            s_z, s_x, s_bc = slots
            s_w, c0 = (s_x, ct * 128) if ct < 4 else (s_bc, (ct - 4) * 128)
            bk = 2 + nbank()
            for kc in range(8):
                MM(P[bk][:, 0:n], wsl[s_w][:, kc, c0:c0 + 128], uT[:, kc, o:o + n], kc == 0, kc == 7, [("wsl", s_w), ("wsl2", s_w)] + ut_reads(o, n), bk)
            OP("dve", "tensor_copy", reads=[("P", bk)], writes=[("pre", ct, o)], banks=[bk], out=pre[:, ct, 3 + o: 3 + o + n], in_=P[bk][:, 0:n])

        def inproj_z(slots):
            s = slots[0]
            for i in range(1, 17):
                o = toff(i)
                bk = 2 + nbank()
                for kc in range(8):
                    MM(P[bk][:, :], uT[:, kc, o:o + 128], wsl[s][:, kc, :], kc == 0, kc == 7, [("wsl", s), ("uT", i)], bk)
                OP("act", "activation", reads=[("P", bk)], writes=[("zs", i)], banks=[bk], out=zs[:, i - 1, :], in_=P[bk][:, :], func=AF.Silu)

        def inproj(g):
            slots = inproj_w(g)
            for ct in range(6):
                for (o, n) in GRP:
                    inproj_x_unit(slots, ct, o, n)
            inproj_z(slots)

        uT = carve(0, [128, 8, T], BF16)
        A0 = 33 * KB
        xts = [carve(A0 + i * 4 * KB, [128, 1024], F32) for i in range(3)]
        ubs = [carve(A0 + 12 * KB + i * 2 * KB, [128, 1024], BF16) for i in range(2)]
        nw1_bc = carve(A0 + 16 * KB, [128, 1024], F32)
        DMA("sp", "k_nw1", nw1_bc, norm1_w[0:1, :].partition_broadcast(128), writes=["nwbc"])

        def a_load(i):
            srcd = meta[:, :] if i == 0 else x[(i - 1) * 128: i * 128, :]
            DMA("sp", "x%d" % (i % 3), xts[i % 3][:tn(i)], srcd, writes=[("xt", i % 3)])

        slots0 = inproj_w(0)
        pend = []
        a_load(0)
        a_load(1)
        norm_stats(tn(0), xts[0][:tn(0)], [("xt", 0)], 0)
        for i in range(17):
            if i + 2 < 17:
                a_load(i + 2)
            if i + 1 < 17:
                j = i + 1
                norm_stats(tn(j), xts[j % 3][:tn(j)], [("xt", j % 3)], j)
            n = tn(i)
            norm_scale(n, xts[i % 3][:n], [("xt", i % 3)], nw1_bc, "nwbc", ubs[i % 2], ("ub", i % 2), i)
            for _ in range(2):
                if pend:
                    inproj_x_unit(slots0, *pend.pop(0))
            norm_tr(n, ubs[i % 2], ("ub", i % 2), i % 2, uT[:, :, toff(i): toff(i) + n], [("uT", i)])
            if i == 0 or i % 4 == 0:
                o_, n_ = GRP[i // 4]
                pend += [(ct, o_, n_) for ct in range(6)]
        while pend:
            inproj_x_unit(slots0, *pend.pop(0))
        inproj_z(slots0)
        if "uT" in dbg:
            tmpf = carve(140 * KB, [128, T], F32)
            OP("dve", "tensor_copy", reads=[("uT", i) for i in range(17)], writes=["tmpf"], out=tmpf, in_=uT[:, 0, :])
            dbg_dump("uT", tmpf, ["tmpf"])
        if stop_after == "A":
            finish()
            return nc

        DMA("sp", "k0", cws, cw_s[:, :, :], writes=["cws"])
        DMA("sp", "k1", cbcol, cb_s_col[:, :], writes=["cbcol"])
        DMA("sp", "k2", dtb_bc, dt_bias[0:1, :].partition_broadcast(128), writes=["dtb"])
        DMA("sp", "k3", alog_bc, a_log[0:1, :].partition_broadcast(128), writes=["alog"])
        DMA("sp", "k4", d_bc, d_skip[0:1, :].partition_broadcast(128), writes=["dbc"])
        DMA("sp", "k5", nws_bc, ssd_nw[0:1, :].partition_broadcast(128), writes=["nwsbc"])
        DMA("pool", "k6", wdt, w_in[:, C_DT:C_DT + 16].rearrange("(kc p) n -> p kc n", p=128), writes=["wdt"])
        DMA("pool", "k7", cbrow_bf, cb_s_row[0:1, :], writes=["cbrow_bf"])
        OP("pool", "memset", writes=["onesX"], ap=onesX, constant=1.0)
        OP("pool", "memset", writes=["sel2"], ap=sel2, constant=0.0)
        OP("pool", "memset", writes=["csT2"], ap=csT2, constant=0.0)
        OP("pool", "memset", writes=["dta3"], ap=dta3, constant=0.0)
        for blk in (0, 32, 64):
            OP("pool", "affine_select", reads=["onesX", "sel2"], writes=["sel2"], out=sel2[blk:blk + 16],
               in_=onesX[blk:blk + 16].rearrange("p (a b) -> p a b", a=16), pattern=[[-1, 16], [0, 128]],
               compare_op=ALU.is_equal, fill=0.0, base=0, channel_multiplier=1)
        for cc in range(12):
            for k in range(4):
                OP("dve", "tensor_scalar", reads=["ident_bf", "cws"], writes=[("diag_s", cc)], out=diag_s[:, cc, k, :], in0=ident_bf[:],
                   scalar1=cws[:, cc, k:k + 1], scalar2=None, op0=ALU.mult)
        for h in range(16):
            OP("dve", "tensor_scalar", reads=["ident_bf", "dbc"], writes=["DI"], out=DI[:, h, :], in0=ident_bf[:], scalar1=d_bc[:, h:h + 1],
               scalar2=None, op0=ALU.mult)
        OP("act", "activation", reads=["alog"], writes=["a_bc0"], out=a_bc, in_=alog_bc, func=AF.Exp)
        OP("dve", "tensor_scalar", reads=["a_bc0"], writes=["a_bc"], out=a_bc, in0=a_bc, scalar1=-1.0, scalar2=None, op0=ALU.mult)

        PDT, PCS, PTOT, PCT = 2, 3, 4, 5
        for i in range(17):
            o = toff(i)
            for kc in range(8):
                MM(P[PDT][:, i * 16:(i + 1) * 16], uT[:, kc, o:o + 128], wdt[:, kc, :], kc == 0, kc == 7, ["wdt"] + ut_reads(o, 128), PDT)
        v3 = lambda ap: ap.rearrange("p (a b) -> p a b", a=17)
        pdt3 = v3(P[PDT][:, 0:272])
        bc17 = lambda ap: ap.unsqueeze(1).to_broadcast([128, 17, 16])
        OP("dve", "tensor_tensor", reads=[("P", PDT), "dtb"], writes=["xb"], banks=[PDT], out=sm["xb"], in0=pdt3, in1=bc17(dtb_bc), op=ALU.add)
        OP("dve", "scalar_tensor_tensor", reads=["xb"], writes=["ax"], out=sm["ax"], in0=sm["xb"], scalar=-1.0, in1=sm["xb"], op0=ALU.mult, op1=ALU.max)
        OP("act", "activation", reads=["ax"], writes=["ex"], out=sm["ex"], in_=sm["ax"], func=AF.Exp, scale=-1.0)
        OP("act", "activation", reads=["ex"], writes=["ex"], out=sm["ex"], in_=sm["ex"], func=AF.Ln, bias=1.0)
        OP("dve", "scalar_tensor_tensor", reads=["xb", "ex"], writes=["dt"], out=sm["dt"], in0=sm["xb"], scalar=0.0, in1=sm["ex"],
           op0=ALU.max, op1=ALU.add)
        OP("dve", "tensor_tensor", reads=["dt", "a_bc"], writes=["dta"], out=sm["dta"], in0=sm["dt"], in1=bc17(a_bc), op=ALU.mult)
        for blk in (0, 32, 64):
            OP("dve", "tensor_copy", reads=["dta", "dta3"], writes=["dta3"], out=dta3[:, :, blk:blk + 16], in_=sm["dta"])
        for c in range(17):
            MM(P[PCS][:, c * 16:(c + 1) * 16], tri_f[:, :], sm["dta"][:, c, :], True, True, ["tri_f", "dta"], PCS)
            MM(P[PTOT][:, c * 16:(c + 1) * 16], ones_f[:tn(c), :], sm["dta"][:tn(c), c, :], True, True, ["ones_f", "dta"], PTOT)
        pcs3, ptot3 = v3(P[PCS][:, 0:272]), v3(P[PTOT][:, 0:272])
        OP("dve", "tensor_copy", reads=[("P", PCS)], writes=["cs"], banks=[PCS], out=sm["cs"], in_=pcs3)
        OP("act", "activation", reads=[("P", PCS)], writes=["ecs"], banks=[PCS], out=sm["ecs"], in_=pcs3, func=AF.Exp)
        OP("act", "activation", reads=[("P", PTOT)], writes=["cd"], banks=[PTOT], out=sm["cd"], in_=ptot3, func=AF.Exp)
        OP("dve", "tensor_tensor", reads=[("P", PTOT), "cs"], writes=["tmp"], banks=[PTOT], out=sm["tmp"], in0=ptot3, in1=sm["cs"], op=ALU.subtract)
        OP("dve", "tensor_scalar", reads=["tmp"], writes=["tmp"], out=sm["tmp"], in0=sm["tmp"], scalar1=0.0, scalar2=None, op0=ALU.min)
        OP("act", "activation", reads=["tmp"], writes=["dec"], out=sm["dec"], in_=sm["tmp"], func=AF.Exp)
        OP("dve", "tensor_tensor", reads=["dec", "dt"], writes=["wdec"], out=sm["wdec"], in0=sm["dec"], in1=sm["dt"], op=ALU.mult)
        for c0 in range(0, 17, 4):
            cl = list(range(c0, min(17, c0 + 4)))
            nn = len(cl)
            for j, c in enumerate(cl):
                MM(P[PCT][:80, j * 128:(j + 1) * 128], dta3[:, c, :], tri_f[:, :], True, True, ["dta3", "tri_f"], PCT)
            srcp = P[PCT][:80, 0:nn * 128].rearrange("p (a b) -> p a b", a=nn)
            OP("act", "activation", reads=[("P", PCT)], writes=["hiA"], banks=[PCT], out=hiA[:, 0:nn, :], in_=srcp, func=AF.Copy)
            OP("dve", "tensor_tensor", reads=[("P", PCT), "hiA"], writes=["r1"], banks=[PCT], out=r1[:, 0:nn, :], in0=srcp, in1=hiA[:, 0:nn, :],
               op=ALU.subtract)
            OP("act", "activation", reads=["r1"], writes=["midA"], out=midA[:, 0:nn, :], in_=r1[:, 0:nn, :], func=AF.Copy)
            OP("dve", "tensor_tensor", reads=["r1", "midA"], writes=["r2"], out=r2[:, 0:nn, :], in0=r1[:, 0:nn, :], in1=midA[:, 0:nn, :], op=ALU.subtract)
            OP("act", "activation", reads=["r2"], writes=["loA"], out=loA[:, 0:nn, :], in_=r2[:, 0:nn, :], func=AF.Copy)
            OP("dve", "tensor_copy", reads=["hiA", "csT2"], writes=["csT2"], out=csT2[0:16, c0:c0 + nn, :], in_=hiA[0:16, 0:nn, :])
            OP("dve", "tensor_copy", reads=["midA", "csT2"], writes=["csT2"], out=csT2[32:48, c0:c0 + nn, :], in_=midA[32:48, 0:nn, :])
            OP("dve", "tensor_copy", reads=["loA", "csT2"], writes=["csT2"], out=csT2[64:80, c0:c0 + nn, :], in_=loA[64:80, 0:nn, :])
        OP("dve", "tensor_scalar", reads=["csT2"], writes=["ncsT2"], out=ncsT2, in0=csT2, scalar1=-1.0, scalar2=None, op0=ALU.mult)
        dbg_dump("dt", sm["dt"].rearrange("p a b -> p (a b)"), ["dt"])
        dbg_dump("cs", sm["cs"].rearrange("p a b -> p (a b)"), ["cs"])

        wsl.append(carve(TQ, [128, 8, 512], BF16))
        for g in range(2):
            h0 = g * 8
            if g == 1:
                S.barrier()
                inproj(1)
            S.barrier()
            for ct, dst, nm in ((4, BT, "BT"), (5, CT, "CT")):
                ccg = (8 + g) if ct == 4 else (10 + g)
                for (o, n) in GRP:
                    bk = nbank()
                    for k in range(4):
                        MM(P[bk][:, 0:n], diag_s[:, ccg, k, :], pre[:, ct, o + k: o + k + n], k == 0, k == 3, [], bk)
                    OP("act", "activation", reads=[("P", bk)], writes=[(nm, o)], banks=[bk], out=dst[:, o:o + n], in_=P[bk][:, 0:n],
                       func=AF.Silu, bias=cbcol[:, ccg:ccg + 1])
            for c in range(17):
                n, o = tn(c), toff(c)
                bk = nbank()
                for j in range(4):
                    ccg = g * 4 + j
                    for k in range(4):
                        MM(P[bk][:n, j * 128:(j + 1) * 128], pre[:, j, o + k: o + k + n], diag_s[:, ccg, k, :], k == 0, False, [], bk)
                    MM(P[bk][:n, j * 128:(j + 1) * 128], ones_bf[0:1, 0:n], cbrow_bf[0:1, ccg * 128:(ccg + 1) * 128], False, True, [], bk)
                OP("act", "activation", reads=[("P", bk)], writes=[("xs", c)], banks=[bk], out=xs_all[:n, c, :], in_=P[bk][:n, :], func=AF.Silu)
                bk = nbank()
                ccg = 8 + g
                for k in range(4):
                    MM(P[bk][:n, 0:128], pre[:, 4, o + k: o + k + n], diag_s[:, ccg, k, :], k == 0, False, [], bk)
                MM(P[bk][:n, 0:128], ones_bf[0:1, 0:n], cbrow_bf[0:1, ccg * 128:(ccg + 1) * 128], False, True, [], bk)
                OP("act", "activation", reads=[("P", bk)], writes=[("btm", c)], banks=[bk], out=btm_all[:n, c, :], in_=P[bk][:n, 0:128], func=AF.Silu)
            S.barrier()
            if g == 0 and "BT" in dbg:
                tmpf3 = carve(LW + 9 * KB, [128, T], F32)
                OP("dve", "tensor_copy", writes=["tmpf3"], out=tmpf3, in_=BT[:, :])
                dbg_dump("BT", tmpf3, ["tmpf3"])
                S.barrier()
            if g == 0:
                DMA("pool", "w3", wsl[3][:, :, :], w_in[:, C_X + 512: C_X + 1024].rearrange("(kc p) n -> p kc n", p=128), writes=[("wsl", 3)])
            PB, PS_, PY, PT_ = 0, 1, 6, 7
            PO = (2, 3)
            PSG = (4, 5)

            def x_cbt(c):
                if c >= 1:
                    o = toff(c)
                    MM(P[PB][:, (c % 2) * 128:(c % 2 + 1) * 128], BT[:, o:o + 128], CT[:, o:o + 128], True, True, [], PB)

            def x_mm(c, with_cbt=True):
                n, o = tn(c), toff(c)
                if with_cbt:
                    x_cbt(c)
                if c < 16:
                    OP("pool", "tensor_tensor", reads=["wdec"], writes=["xw"], out=h8(xw_sb[:n]), in0=h8(xs_all[:n, c, :]),
                       in1=sm["wdec"][:n, c, h0:h0 + 8].unsqueeze(2).to_broadcast([n, 8, 64]), op=ALU.mult)
                    MM(P[PS_][:, :], btm_all[:n, c, :], xw_sb[:n, :], True, True, ["xw"], PS_)
                if c >= 1:
                    MM(P[PO[c % 2]][:, :], CT[:, o:o + 128], prev_bf[:, :], True, True, ["prev_bf"], PO[c % 2])
                    OP("pool", "tensor_tensor", reads=["dt"], writes=[("xdt", c % 2)], out=h8(xdt_sb[c % 2][:n]), in0=h8(xs_all[:n, c, :]),
                       in1=sm["dt"][:n, c, h0:h0 + 8].unsqueeze(2).to_broadcast([n, 8, 64]), op=ALU.mult)

            def prev_chain(c):
                if c >= 16:
                    return
                if c == 0:
                    OP("dve", "tensor_copy", reads=[("P", PS_)], writes=["prev32"], banks=[PS_], out=prev32, in_=P[PS_][:, :])
                else:
                    OP("pool", "tensor_tensor", reads=["prev32", "cd"], writes=["prev32"], out=h8(prev32), in0=h8(prev32),
                       in1=sm["cd"][:, c, h0:h0 + 8].unsqueeze(2).to_broadcast([128, 8, 64]), op=ALU.mult)
                    OP("dve", "tensor_tensor", reads=["prev32", ("P", PS_)], writes=["prev32"], banks=[PS_], out=prev32, in0=prev32,
                       in1=P[PS_][:, :], op=ALU.add)
                OP("act", "activation", reads=["prev32"], writes=["prev_bf"], out=prev_bf, in_=prev32, func=AF.Copy)

            def look_seg(c):
                for hb in range(2):
                    bk = PSG[hb]
                    for hh in range(4):
                        h = h0 + hb * 4 + hh
                        osl = P[bk][:, hh * 128:(hh + 1) * 128]
                        MM(osl, sel2[:, h, :], csT2[:, c, :], True, False, ["sel2", "csT2"], bk)
                        MM(osl, ncsT2[:, c, :], sel2[:, h, :], False, False, ["sel2", "ncsT2"], bk)
                        MM(osl, ident_bf[:, :], mneg_bf[:, :], False, True, ["ident_bf", "mneg_bf"], bk)
                    e = es_sb[(c % 2) * 2 + hb]
                    OP("act", "activation", reads=[("P", bk)], writes=[("es", c % 2, hb)], banks=[bk], out=e, in_=P[bk][:, :], func=AF.Exp)

            def look_M(c):
                for hb in range(2):
                    OP("dve", "tensor_tensor", reads=[("es", c % 2, hb), ("P", PB)], writes=[("M", c % 2, hb)], banks=[PB], out=M_sb[(c % 2) * 2 + hb],
                       in0=es_sb[(c % 2) * 2 + hb].rearrange("p (a b) -> p a b", a=4),
                       in1=P[PB][:, (c % 2) * 128:(c % 2 + 1) * 128].unsqueeze(1).to_broadcast([128, 4, 128]), op=ALU.mult)

            def y_diag(c):
                for hb in range(2):
                    for hh in range(4):
                        hl = hb * 4 + hh
                        MM(P[PY][:, hl * 64:(hl + 1) * 64], M_sb[(c % 2) * 2 + hb][:, hh, :], xdt_sb[c % 2][:, hl * 64:(hl + 1) * 64], True, False,
                           [("M", c % 2, hb), ("xdt", c % 2)], PY)
                        MM(P[PY][:, hl * 64:(hl + 1) * 64], DI[:, h0 + hl, :], xs_all[:, c, hl * 64:(hl + 1) * 64], False, True, ["DI"], PY)

            def z1a(c):
                OP("dve", "tensor_tensor", reads=[("P", PO[c % 2]), "ecs"], writes=["t1"], banks=[PO[c % 2]], out=h8(t1_sb),
                   in0=h8(P[PO[c % 2]][:, :]), in1=sm["ecs"][:, c, h0:h0 + 8].unsqueeze(2).to_broadcast([128, 8, 64]), op=ALU.mult)
                OP("dve", "tensor_tensor", reads=["t1", ("P", PY)], writes=["t1"], banks=[PY], out=t1_sb, in0=t1_sb, in1=P[PY][:, :], op=ALU.add)

            def z1b(c):
                OP("pool", "tensor_tensor", reads=["t1", ("zs", c)], writes=["t1"], out=t1_sb, in0=t1_sb, in1=zs[:, c - 1, :], op=ALU.mult)
                ss, lnv, rs = stat[:, 3, c:c + 1], stat[:, 4, c:c + 1], stat[:, 5, c:c + 1]
                OP("act", "activation", reads=["t1"], writes=["junk", ("sg0", c)], out=junk[:, 0:512], in_=t1_sb, func=AF.Square, accum_out=ss)
                OP("act", "activation", reads=[("sg0", c)], writes=[("sg1", c)], out=lnv, in_=ss, func=AF.Ln, scale=1.0 / 512, bias=EPS)
                OP("act", "activation", reads=[("sg1", c)], writes=[("sg2", c)], out=rs, in_=lnv, func=AF.Exp, scale=-0.5)

            def z2(c):
                rs = stat[:, 5, c:c + 1]
                OP("dve", "scalar_tensor_tensor", reads=["t1", ("sg2", c), "nwsbc"], writes=["gn"], out=gn_sb, in0=t1_sb, scalar=rs,
                   in1=nws_bc[:, g * 512:(g + 1) * 512], op0=ALU.mult, op1=ALU.mult)
                Ptb = P[PT_][:].bitcast(BF16)
                for j in range(4):
                    S.op("pe", lambda e, j=j, Ptb=Ptb: e.transpose(out=Ptb[:, j * 128:(j + 1) * 128], in_=gn_sb[:, j * 128:(j + 1) * 128],
                                                                   identity=ident_bf[:, :]), ["gn", "ident_bf"], [("P", PT_)], [PT_])
                OP("act", "activation", reads=[("P", PT_)], writes=[("yT_ssd", g, c)], banks=[PT_],
                   out=yT_ssd[:, g * 4:(g + 1) * 4, (c - 1) * 128: c * 128], in_=Ptb[:, 0:512].rearrange("p (a b) -> p a b", a=4), func=AF.Copy)

            x_mm(0)
            prev_chain(0)
            x_mm(1)
            look_seg(1)
            look_M(1)
            prev_chain(1)
            for k in range(1, 17):
                if k >= 2:
                    z1a(k - 1)
                if k + 1 <= 16:
                    x_cbt(k + 1)
                    look_seg(k + 1)
                    x_mm(k + 1, with_cbt=False)
                y_diag(k)
                if k >= 2:
                    z1b(k - 1)
                if k + 1 <= 16:
                    look_M(k + 1)
                    prev_chain(k + 1)
                if k >= 2:
                    z2(k - 1)
            z1a(16)
            z1b(16)
            z2(16)
        S.barrier()
        if "yT_ssd" in dbg:
            tmpf4 = carve(B0, [128, 8, SEQ], F32)
            OP("dve", "tensor_copy", writes=["tmpf4"], out=tmpf4, in_=yT_ssd)
            dbg_dump("yT_ssd", tmpf4.rearrange("p a b -> p (a b)"), ["tmpf4"])
            S.barrier()
        if stop_after == "B":
            finish()
            return nc

        gate_sb = carve(97 * KB, [128, 8, SEQ], BF16)
        xpre = carve(129 * KB, [128, 8, T + 3], BF16)
        WC0 = 162 * KB
        wsc = [carve(WC0 + i * 8 * KB, [128, 8, 512], BF16) for i in range(3)]
        CC = 186 * KB
        diag_l = carve(CC, [128, 8, 4, 128], BF16)
        bd_a = carve(CC + 8 * KB, [128, 8, 128], BF16)
        bd_x = carve(CC + 10 * KB, [128, 8, 128], BF16)
        lv = carve(CC + 12 * KB, [128, 8, 9], F32)
        lvh = carve(CC + 12 * KB + 512, [128, 8, 9], F32)
        hc8 = carve(CC + 12 * KB + 832, [128, 8], F32)
        c8 = carve(CC + 12 * KB + 288, [128, 8], F32)
        hlast = carve(CC + 12 * KB + 320, [128, 8], F32)
        lt = [carve(CC + 12 * KB + 352 + 32 * k, [128, 8], F32) for k in range(3)]
        DMA("sp", "k_lv", lv, lvec[:, :, :], writes=["lv"])
        OP("pool", "memset", writes=["bd_a"], ap=bd_a, constant=0.0)
        OP("pool", "memset", writes=["bd_x"], ap=bd_x, constant=0.0)
        OP("pool", "memset", writes=["xprepad"], ap=xpre[:, :, 0:3], constant=0.0)
        for wsrc, bd, nm in ((lru_wa, bd_a, "bd_a"), (lru_wx, bd_x, "bd_x")):
            wv = wsrc.rearrange("(c two) i j -> two i c j", two=2)
            for par in range(2):
                DMA("pool", "kb_%s%d" % (nm, par), bd[par * 64:(par + 1) * 64, :, par * 64:(par + 1) * 64], wv[par], reads=[nm],
                    writes=[nm + "w%d" % par])
        BD = ["bd_a", "bd_x", "bd_aw0", "bd_aw1", "bd_xw0", "bd_xw1"]
        for cc in range(8):
            for k in range(4):
                OP("dve", "tensor_scalar", reads=["ident_bf", "lv"], writes=[("diag_l", cc)], out=diag_l[:, cc, k, :], in0=ident_bf[:],
                   scalar1=lv[:, cc, k:k + 1], scalar2=None, op0=ALU.mult)
        OP("dve", "tensor_scalar", reads=["lv"], writes=["lvh"], out=lvh, in0=lv, scalar1=0.5, scalar2=None, op0=ALU.mult)
        lam = lv[:, :, 7]
        OP("dve", "scalar_tensor_tensor", reads=["lv"], writes=["lt0"], out=lt[0], in0=lam, scalar=-1.0, in1=lam, op0=ALU.mult, op1=ALU.max)
        OP("act", "activation", reads=["lt0"], writes=["lt0"], out=lt[0], in_=lt[0], func=AF.Exp, scale=-1.0)
        OP("act", "activation", reads=["lt0"], writes=["lt0"], out=lt[0], in_=lt[0], func=AF.Ln, bias=1.0)
        OP("dve", "tensor_scalar", reads=["lv"], writes=["lt1"], out=lt[1], in0=lam, scalar1=-1.0, scalar2=0.0, op0=ALU.mult, op1=ALU.max)
        OP("dve", "tensor_tensor", reads=["lt0", "lt1"], writes=["lt2"], out=lt[2], in0=lt[0], in1=lt[1], op=ALU.add)
        OP("dve", "tensor_scalar", reads=["lt2"], writes=["c8"], out=c8, in0=lt[2], scalar1=-8.0, scalar2=None, op0=ALU.mult)
        OP("dve", "tensor_scalar", reads=["lt2"], writes=["hc8"], out=hc8, in0=lt[2], scalar1=-4.0, scalar2=None, op0=ALU.mult)
        for part in range(4):
            base = (C_LG if part < 2 else C_LX) + (part % 2) * 512
            s = wload(w_in[:, base: base + 512], wsc)
            for ctl in range(4):
                cc = (part % 2) * 4 + ctl
                for gi, (o, n) in enumerate(GRP):
                    if part < 2 and gi == 0:
                        continue
                    bk = nbank()
                    for kc in range(8):
                        MM(P[bk][:, 0:n], wsc[s][:, kc, ctl * 128:(ctl + 1) * 128], uT[:, kc, o:o + n], kc == 0, kc == 7, [("wsl", s)], bk)
                    if part < 2:
                        OP("act", "activation", reads=[("P", bk)], writes=[("gate", cc, gi)], banks=[bk], out=gate_sb[:, cc, o - 16: o - 16 + n],
                           in_=P[bk][:, 0:n], func=AF.Gelu_apprx_tanh)
                    else:
                        OP("dve", "tensor_copy", reads=[("P", bk)], writes=[("xpre", cc, gi)], banks=[bk], out=xpre[:, cc, 3 + o: 3 + o + n],
                           in_=P[bk][:, 0:n])
        S.barrier()
        yT_lru = gate_sb
        rA = [carve(i * 2 * KB, [128, 512], F32) for i in range(8)]
        iB = [carve(16 * KB + i * 2 * KB, [128, 512], F32) for i in range(8)]
        qq = [carve(65 * KB + i * 2 * KB, [128, 512], F32) for i in range(8)]
        xr32 = [carve(81 * KB + i * 2 * KB, [128, 512], F32) for i in range(8)]
        bd_af = carve(162 * KB, [128, 8, 128], F32)
        bd_xf = carve(166 * KB, [128, 8, 128], F32)
        OP("pool", "memset", writes=["bd_af"], ap=bd_af, constant=0.0)
        OP("pool", "memset", writes=["bd_xf"], ap=bd_xf, constant=0.0)
        for wsrc, bdf, nm in ((lru_wa, bd_af, "bd_af"), (lru_wx, bd_xf, "bd_xf")):
            wv = wsrc.rearrange("(c two) i j -> two i c j", two=2)
            for par in range(2):
                DMA("sp", "kf_%s%d" % (nm, par), bdf[par * 64:(par + 1) * 64, :, par * 64:(par + 1) * 64], wv[par], reads=[nm],
                    writes=[nm + "w%d" % par])
        BDF = ["bd_af", "bd_xf", "bd_afw0", "bd_afw1", "bd_xfw0", "bd_xfw1"]
        h_sb = [carve(170 * KB + i * 2 * KB, [128, 512], F32) for i in range(2)]
        ysq = [carve(174 * KB + i * KB, [128, 512], BF16) for i in range(4)]
        ssrow = carve(178 * KB, [1, 512], F32)
        PSS, PC = 6, 7
        ssteps = [(gi, o, n, q4) for gi, (o, n) in enumerate(GRP) for q4 in range(2)]
        sub = [0]

        def f_sub(s, j):
            gi, o, n, q4 = ssteps[s]
            cc, b = q4 * 4 + j, (s % 2) * 4 + j
            p = (s * 4 + j) % 2
            bx, br, bi = p, 2 + p, 4 + p
            for kk in range(4):
                MM(P[bx][:, 0:n], diag_l[:, cc, kk, :], xpre[:, cc, o + kk: o + kk + n], kk == 0, kk == 3, [], bx)
            OP("dve", "tensor_scalar", reads=[("P", bx)], writes=[("xr32", b)], banks=[bx], out=xr32[b][:, :n], in0=P[bx][:, 0:n],
               scalar1=lv[:, cc, 4:5], scalar2=None, op0=ALU.add)

        def f_gate(s, j):
            gi, o, n, q4 = ssteps[s]
            cc, b = q4 * 4 + j, (s % 2) * 4 + j
            p = (s * 4 + j) % 2
            br, bi = 2 + p, 4 + p
            MM(P[br][:, 0:n], bd_af[:, cc, :], xr32[b][:, :n], True, True, [("xr32", b)] + BDF, br)
            MM(P[bi][:, 0:n], bd_xf[:, cc, :], xr32[b][:, :n], True, True, [("xr32", b)] + BDF, bi)

        def f_act(s, j):
            gi, o, n, q4 = ssteps[s]
            cc, b = q4 * 4 + j, (s % 2) * 4 + j
            p = (s * 4 + j) % 2
            br, bi = 2 + p, 4 + p
            OP("act", "activation", reads=[("P", br)], writes=[("rA", b)], banks=[br], out=rA[b][:, :n], in_=P[br][:, 0:n], func=AF.Tanh,
               scale=0.5, bias=lvh[:, cc, 5:6])
            OP("act", "activation", reads=[("P", bi)], writes=[("iB", b)], banks=[bi], out=iB[b][:, :n], in_=P[bi][:, 0:n], func=AF.Tanh,
               scale=0.5, bias=lvh[:, cc, 6:7])
            OP("act", "activation", reads=[("rA", b)], writes=[("qq", b)], out=qq[b][:, :n], in_=rA[b][:, :n], func=AF.Exp,
               scale=c8[:, cc:cc + 1], bias=c8[:, cc:cc + 1])
            OP("act", "activation", reads=[("rA", b)], writes=[("rA", b)], out=rA[b][:, :n], in_=rA[b][:, :n], func=AF.Exp,
               scale=hc8[:, cc:cc + 1], bias=hc8[:, cc:cc + 1])
            OP("dve", "scalar_tensor_tensor", reads=[("iB", b), ("xr32", b)], writes=[("iB", b)], out=iB[b][:, :n], in0=iB[b][:, :n], scalar=1.0,
               in1=xr32[b][:, :n], op0=ALU.add, op1=ALU.mult)

        def s_mid(s):
            gi, o, n, q4 = ssteps[s]
            bs = [(s % 2) * 4 + j for j in range(4)]
            for b in bs:
                OP("act", "activation", reads=[("qq", b)], writes=[("qq", b)], out=qq[b][:, :n], in_=qq[b][:, :n], func=AF.Sqrt, scale=-0.25, bias=0.25)
            for b in bs:
                OP("pool", "tensor_tensor", reads=[("iB", b), ("qq", b)], writes=[("iB", b)], out=iB[b][:, :n], in0=iB[b][:, :n], in1=qq[b][:, :n],
                   op=ALU.mult)

        def b_sub(s, j):
            gi, o, n, q4 = ssteps[s]
            cc, b = q4 * 4 + j, (s % 2) * 4 + j
            hb, hk = h_sb[j % 2], ("h", j % 2)
            init = 0.0 if gi == 0 else hlast[:, cc:cc + 1]
            OP("dve", "tensor_tensor_scan", reads=[("rA", b), ("iB", b), ("hlast", cc)], writes=[hk], out=hb[:, :n], data0=rA[b][:, :n],
               data1=iB[b][:, :n], initial=init, op0=ALU.mult, op1=ALU.add)
            if gi < 4:
                OP("dve", "tensor_copy", reads=[hk], writes=[("hlast", cc)], out=hlast[:, cc:cc + 1], in_=hb[:, n - 1:n])
            if gi >= 1:
                gsl = gate_sb[:, cc, o - 16: o - 16 + n]
                OP("dve", "tensor_tensor", reads=[hk, ("gate", cc, gi)], writes=[hk], out=hb[:, :n], in0=hb[:, :n], in1=gsl, op=ALU.mult)
                OP("pool", "tensor_tensor", reads=[hk], writes=[("ysq", j)], out=ysq[j][:, :n], in0=hb[:, :n], in1=hb[:, :n], op=ALU.mult)
                OP("pool", "tensor_scalar", reads=[hk, ("gate", cc, gi)], writes=[("gate", cc, gi)], out=gsl, in0=hb[:, :n],
                   scalar1=lv[:, cc, 8:9], scalar2=1.0, op0=ALU.mult, op1=ALU.mult)

        def s_ss(s):
            gi, o, n, q4 = ssteps[s]
            if gi == 0:
                return
            for j in range(4):
                cc = q4 * 4 + j
                MM(P[PSS][0:1, 0:n], ones_bf[:, 0:1], ysq[j][:, :n], cc == 0, cc == 7, [("ysq", j)], PSS)
            if q4 == 1:
                OP("dve", "tensor_copy", reads=[("P", PSS)], writes=["ssrow"], banks=[PSS], out=ssrow[0:1, :], in_=P[PSS][0:1, :])
                for j in range(4):
                    MM(P[PC][:, j:j + 1], ssrow[0:1, j * 128:(j + 1) * 128], ones_f[0:1, 0:1], True, True, ["ssrow"], PC)
                OP("act", "activation", reads=[("P", PC)], writes=["sl1"], banks=[PC], out=stat[:, 6, 0:4], in_=P[PC][:, 0:4], func=AF.Ln,
                   scale=1.0 / 1024, bias=EPS)
                OP("act", "activation", reads=["sl1"], writes=["rstd_l"], out=rstd_l[:, (gi - 1) * 4: gi * 4], in_=stat[:, 6, 0:4], func=AF.Exp,
                   scale=-0.5)

        NS = len(ssteps)
        f_sub(0, 0)
        for j in range(4):
            if j + 1 < 4:
                f_sub(0, j + 1)
            f_gate(0, j)
            f_act(0, j)
        s_mid(0)
        wo_sb = carve(65 * KB, [128, 16, 1024], BF16)

        def wo_prefetch(par):
            for base, nm in ((0, "qq"), (8, "xr32")):
                c0 = base + par * 4
                DMA("pool", "wo%d" % (par * 2 + (base // 8)), wo_sb[:, c0:c0 + 4, :],
                    w_out[c0 * 128:(c0 + 4) * 128, :].rearrange("(c p) n -> p c n", p=128),
                    writes=[(nm, par * 4 + j) for j in range(4)])

        for s in range(NS):
            if s == NS - 2:
                wo_prefetch(s % 2)
            if s == NS - 1:
                wo_prefetch(s % 2)
            for j in range(4):
                if s + 1 < NS:
                    f_sub(s + 1, j)
                    if j >= 1:
                        f_gate(s + 1, j - 1)
                b_sub(s, j)
                if s + 1 < NS and j >= 2:
                    f_act(s + 1, j - 2)
            if s + 1 < NS:
                f_gate(s + 1, 3)
                f_act(s + 1, 2)
                f_act(s + 1, 3)
                s_mid(s + 1)
            s_ss(s)
        S.barrier()
        if "yT_lru" in dbg:
            tmpf5 = carve(0, [128, 4, SEQ], F32)
            OP("dve", "tensor_copy", writes=["tmpf5"], out=tmpf5, in_=yT_lru[:, 0:4, :])
            dbg_dump("yT_lru", tmpf5.rearrange("p a b -> p (a b)"), ["tmpf5"])
            dbg_dump("rstd_l", rstd_l[:, :], [])
            S.barrier()
        if stop_after == "C":
            finish()
            return nc

        h1 = carve(129 * KB, [128, 16, 1024], F32)
        u2T = carve(0, [128, 8, SEQ], BF16)
        ub2 = [carve(193 * KB + i * 2 * KB, [128, 1024], BF16) for i in range(2)]
        nw2_bc = carve(197 * KB, [128, 1024], F32)
        DMA("sp", "k_nw2", nw2_bc, norm2_w[0:1, :].partition_broadcast(128), writes=["nw2bc"])

        def d_load(i):
            rd = [("h1", i - 3, 0)] if i >= 3 else []
            DMA("sp", "x%d" % (i % 3), h1[:, i, :], x[i * 128:(i + 1) * 128, :], reads=rd, writes=[("h1", i, 0), ("h1", i, 1)])

        d_load(0)
        d_load(1)
        slot = 0
        H1R = lambda t: [("h1", t, 0), ("h1", t, 1)]

        def n2_stats(t):
            norm_stats(128, h1[:, t, :], H1R(t), t)

        def n2_apply(t):
            norm_apply(128, h1[:, t, :], H1R(t), nw2_bc, "nw2bc", ub2[t % 2], ("ub2", t % 2), 6 + t % 2,
                       u2T[:, :, t * 128:(t + 1) * 128], [("u2T", t)], t)

        for i in range(16):
            if i + 2 < 16:
                d_load(i + 2)
            if i >= 1:
                n2_stats(i - 1)
            banks_i = []
            for dh in range(2):
                bs, bl = (slot % 3) * 2, (slot % 3) * 2 + 1
                slot += 1
                banks_i.append((bs, bl))
                for cc in range(8):
                    MM(P[bs][:, :], yT_ssd[:, cc, i * 128:(i + 1) * 128], wo_sb[:, cc, dh * 512:(dh + 1) * 512], cc == 0, cc == 7, [], bs)
                for cc in range(8):
                    MM(P[bl][:, :], yT_lru[:, cc, i * 128:(i + 1) * 128], wo_sb[:, 8 + cc, dh * 512:(dh + 1) * 512], cc == 0, cc == 7, [], bl)
            if i >= 2:
                n2_apply(i - 2)
            for dh in range(2):
                bs, bl = banks_i[dh]
                hsl = h1[:, i, dh * 512:(dh + 1) * 512]
                OP("dve", "tensor_tensor", reads=[("P", bs), ("h1", i, dh)], writes=[("h1", i, dh)], banks=[bs], out=hsl, in0=hsl, in1=P[bs][:, :],
                   op=ALU.add)
                OP("dve", "scalar_tensor_tensor", reads=[("P", bl), ("h1", i, dh)], writes=[("h1", i, dh)], banks=[bl], out=hsl, in0=P[bl][:, :],
                   scalar=rstd_l[:, i:i + 1], in1=hsl, op0=ALU.mult, op1=ALU.add)
        n2_stats(15)
        n2_apply(14)
        n2_apply(15)
        S.barrier()
        if "h1" in dbg:
            dbg_dump("h1", h1.rearrange("p a b -> p (a b)"), [])
            S.barrier()
        if stop_after == "D":
            finish()
            return nc

        nwf_bc = carve(121 * KB, [128, 1024], F32)
        DMA("sp", "k_nwf", nwf_bc, final_nw[0:1, :].partition_broadcast(128), writes=["nwfbc"])
        fslots = [33 * KB, 69 * KB]
        wgs = [carve(b, [128, 8, 768], BF16) for b in fslots]
        wus = [carve(b + 12 * KB, [128, 8, 768], BF16) for b in fslots]
        wds = [carve(b + 24 * KB, [128, 6, 1024], BF16) for b in fslots]
        hff = [carve(105 * KB + i * 6 * KB, [128, 6, 512], BF16) for i in range(2)]
        sgb = [carve(117 * KB + i * 2 * KB, [128, 512], F32) for i in range(2)]
        groups = [(0, 4), (4, 6), (10, 6), (16, 6)]
        NG = len(groups)

        def fload(q):
            f0, nf = groups[q]
            sl = q % 2
            DMA("pool", "fwg%d" % sl, wgs[sl][:, :, 0:nf * 128], w_gate[:, f0 * 128:(f0 + nf) * 128].rearrange("(kc p) n -> p kc n", p=128),
                writes=[("wg", sl)])
            DMA("pool", "fwu%d" % sl, wus[sl][:, :, 0:nf * 128], w_up[:, f0 * 128:(f0 + nf) * 128].rearrange("(kc p) n -> p kc n", p=128),
                writes=[("wu", sl)])
            DMA("pool", "fwd%d" % sl, wds[sl][:, 0:nf, :], w_down[f0 * 128:(f0 + nf) * 128, :].rearrange("(f p) n -> p f n", p=128),
                writes=[("wd", sl)])

        fload(0)
        fload(1)
        step = 0
        for q in range(NG):
            f0, nf = groups[q]
            sl = q % 2
            FW = [("wg", sl), ("wu", sl), ("wd", sl)]
            for tg in range(4):
                hb = hff[tg % 2]
                for f in range(nf):
                    bg, bu = step % 2, 2 + step % 2
                    step += 1
                    for kc in range(8):
                        MM(P[bg][:, :], wgs[sl][:, kc, f * 128:(f + 1) * 128], u2T[:, kc, tg * 512:(tg + 1) * 512], kc == 0, kc == 7, [("wg", sl)], bg)
                    for kc in range(8):
                        MM(P[bu][:, :], wus[sl][:, kc, f * 128:(f + 1) * 128], u2T[:, kc, tg * 512:(tg + 1) * 512], kc == 0, kc == 7, [("wu", sl)], bu)
                    sg = sgb[step % 2]
                    OP("act", "activation", reads=[("P", bg)], writes=[("sg", step % 2)], banks=[bg], out=sg, in_=P[bg][:, :], func=AF.Silu)
                    OP("dve", "tensor_tensor", reads=[("sg", step % 2), ("P", bu)], writes=[("hff", tg % 2, f)], banks=[bu], out=hb[:, f, :], in0=sg,
                       in1=P[bu][:, :], op=ALU.mult)
                for j in range(4):
                    ti = tg * 4 + j
                    for dh in range(2):
                        bd_ = 4 + (j * 2 + dh) % 4
                        for f in range(nf):
                            MM(P[bd_][:, :], hb[:, f, j * 128:(j + 1) * 128], wds[sl][:, f, dh * 512:(dh + 1) * 512], f == 0, f == nf - 1,
                               [("hff", tg % 2, f), ("wd", sl)], bd_)
                        hsl = h1[:, ti, dh * 512:(dh + 1) * 512]
                        OP("dve", "tensor_tensor", reads=[("P", bd_), ("h1", ti, dh)], writes=[("h1", ti, dh)], banks=[bd_], out=hsl, in0=hsl,
                           in1=P[bd_][:, :], op=ALU.add)
                    if q == NG - 1:
                        src = h1[:, ti, :]
                        rd = [("h1", ti, 0), ("h1", ti, 1)]
                        norm_stats(128, src, rd, ti)
                        OP("dve", "scalar_tensor_tensor", reads=rd + [("st2", ti), "nwfbc"], writes=rd, out=src, in0=src,
                           scalar=stat[:, 2, ti:ti + 1], in1=nwf_bc, op0=ALU.mult, op1=ALU.mult)
                        DMA("sp", "out%d" % (ti % 2), out[ti * 128:(ti + 1) * 128, :], src, reads=rd)
            if q + 2 < NG:
                fload(q + 2)
        finish()
    return nc


def host_inputs(inp):
    f = lambda a: np.ascontiguousarray(np.asarray(a, dtype=np.float32))
    cw = f(inp["ssd_conv_w"])[0]
    cw_s = f(cw.reshape(4, 12, 128).transpose(2, 1, 0))
    cb = f(inp["ssd_conv_b"])[0]
    cb_s_col = f(cb.reshape(12, 128).T)
    lw = f(inp["lru_conv_w"])[0]
    col = lambda v: f(v).reshape(8, 128).T
    lvec = np.stack([col(lw[0]), col(lw[1]), col(lw[2]), col(lw[3]), col(inp["lru_conv_b"]), col(inp["lru_ba"]), col(inp["lru_bx"]),
                     col(inp["lru_lambda"]), col(inp["lru_norm_w"])], axis=-1)
    shared = {
        "meta": f(inp["meta_tokens"]), "norm1_w": f(inp["norm1_w"]), "w_in": f(inp["w_in"])[0], "cw_s": cw_s, "cb_s_col": cb_s_col,
        "cb_s_row": f(cb.reshape(1, 1536)), "dt_bias": f(inp["ssd_dt_bias"]), "a_log": f(inp["ssd_a_log"]), "d_skip": f(inp["ssd_d"]),
        "ssd_nw": f(inp["ssd_norm_w"]), "lvec": f(lvec), "lru_wa": f(inp["lru_wa"])[0], "lru_wx": f(inp["lru_wx"])[0],
        "w_out": f(inp["w_out"])[0], "norm2_w": f(inp["norm2_w"]), "w_gate": f(inp["w_gate"])[0], "w_up": f(inp["w_up"])[0],
        "w_down": f(inp["w_down"])[0], "final_nw": f(inp["final_norm_w"]).reshape(1, D),
    }
    xs = f(inp["x"])
    return [dict(shared, x=xs[b]) for b in range(8)]


def kernel(**inputs):
    nc = build()
    in_maps = host_inputs(inputs)
    res = run_bass_kernel_spmd(nc, in_maps, core_ids=list(range(8)))
    return np.stack([np.asarray(r["out"], dtype=np.float32) for r in res.results], axis=0)
```

```python
import numpy as np
from contextlib import ExitStack
import concourse.bass as bass
import concourse.mybir as mybir
from concourse.bass_utils import run_bass_kernel_spmd

F32 = mybir.dt.float32
BF16 = mybir.dt.bfloat16
AF = mybir.ActivationFunctionType
ALU = mybir.AluOpType

ENGS = ("pe", "dve", "act", "pool", "sp")
CENG = ("pe", "dve", "act", "pool")

D = 1024
SEQ = 2048
NMETA = 16
T = SEQ + NMETA
DFF = 2816
NFC = DFF // 128
EPS = 1e-6
C_Z, C_X, C_B, C_C, C_DT, C_LG, C_LX = 0, 1024, 2048, 2304, 2560, 2576, 3600
GRP = [(0, 16)] + [(16 + 512 * j, 512) for j in range(4)]


def toff(i):
    return 0 if i == 0 else 16 + 128 * (i - 1)


def tn(i):
    return 16 if i == 0 else 128


class Sched:
    def __init__(self):
        self.streams = {e: [] for e in ENGS}
        self.nops = {e: 0 for e in ENGS}
        self.known = {e: {} for e in ENGS}
        self.clock = {}
        self.reg = {}
        self.bank = {}
        self.dma_cnt = {}
        self.targets = set()

    def _deps(self, eng, reads, writes, banks):
        deps = []
        for r in reads:
            st = self.reg.get(r)
            if st and st["w"]:
                deps.append((st["w"], "raw"))
        for w in writes:
            st = self.reg.get(w)
            if st:
                if st["w"]:
                    deps.append((st["w"], "waw"))
                for sk, idx in st["r"].items():
                    deps.append(((sk, idx), "war"))
        for b in banks:
            for sk, idx in self.bank.get(b, {}).items():
                if sk != eng:
                    deps.append(((sk, idx), "bank"))
        need = {}
        for (sk, idx), kind in deps:
            if sk == eng:
                if eng in ("pe", "sp"):
                    continue
            if need.get(sk, 0) < idx:
                need[sk] = idx
        return need

    def _apply_waits(self, eng, need):
        kn = self.known[eng]
        waits = [(sk, idx) for sk, idx in need.items() if kn.get(sk, 0) < idx]
        for sk, idx in waits:
            if kn.get(sk, 0) < idx:
                kn[sk] = idx
            ck = self.clock.get((sk, idx))
            if ck:
                for k2, v2 in ck.items():
                    if k2 != eng and kn.get(k2, 0) < v2:
                        kn[k2] = v2
            if not sk.startswith("d:"):
                self.targets.add((sk, idx))
        return waits

    def op(self, eng, fn, reads=(), writes=(), banks=()):
        need = self._deps(eng, reads, writes, banks)
        waits = self._apply_waits(eng, need)
        self.nops[eng] += 1
        idx = self.nops[eng]
        self.clock[(eng, idx)] = dict(self.known[eng])
        self.streams[eng].append([waits, fn, "op", (eng, idx)])
        for r in reads:
            st = self.reg.setdefault(r, {"w": None, "r": {}})
            st["r"][eng] = idx
        for w in writes:
            self.reg[w] = {"w": (eng, idx), "r": {}}
        for b in banks:
            self.bank.setdefault(b, {})[eng] = idx
        return (eng, idx)

    def dma(self, queue, fn, sem, reads=(), writes=()):
        sk = "d:" + sem
        need = self._deps(queue, reads, writes, ())
        waits = self._apply_waits(queue, need)
        self.dma_cnt[sk] = self.dma_cnt.get(sk, 0) + 1
        idx = self.dma_cnt[sk]
        self.clock[(sk, idx)] = dict(self.known[queue])
        self.streams[queue].append([waits, fn, "dma", (sk, idx)])
        for r in reads:
            st = self.reg.setdefault(r, {"w": None, "r": {}})
            st["r"][sk] = idx
        for w in writes:
            self.reg[w] = {"w": (sk, idx), "r": {}}
        return (sk, idx)

    def barrier(self):
        items = [(e, self.nops[e]) for e in CENG if self.nops[e] > 0]
        items += [(sk, c) for sk, c in self.dma_cnt.items()]
        for e in ENGS:
            need = {}
            for sk, idx in items:
                need[sk] = idx
            waits = self._apply_waits(e, need)
            self.streams[e].append([waits, None, "wait", None])
        self.reg = {}
        self.bank = {}

    def final_wait(self, eng, items):
        need = {}
        for sk, idx in items:
            if need.get(sk, 0) < idx:
                need[sk] = idx
        waits = self._apply_waits(eng, need)
        self.streams[eng].append([waits, None, "wait", None])

    def emit(self, nc, stack):
        sems = {}
        for e in CENG:
            sems[e] = stack.enter_context(nc.semaphore("s_" + e))
        for sk in self.dma_cnt:
            sems[sk] = stack.enter_context(nc.semaphore("s_" + sk.replace(":", "_")))
        rank = {}
        for e in CENG:
            ids = sorted(i for (s, i) in self.targets if s == e)
            rank[e] = {i: n + 1 for n, i in enumerate(ids)}

        def run(eng_name, eng):
            for waits, fn, kind, key in self.streams[eng_name]:
                for sk, idx in waits:
                    val = 16 * idx if sk.startswith("d:") else rank[sk][idx]
                    eng.wait_ge(sems[sk], val)
                if fn is None:
                    continue
                ins = fn(eng)
                if kind == "dma":
                    ins.then_inc(sems[key[0]], 16)
                elif key in self.targets:
                    ins.then_inc(sems[key[0]], 1)

        block = stack.enter_context(nc.Block())

        @block.tensor
        def _(e):
            run("pe", e)

        @block.vector
        def _(e):
            run("dve", e)

        @block.scalar
        def _(e):
            run("act", e)

        @block.gpsimd
        def _(e):
            run("pool", e)

        @block.sync
        def _(e):
            run("sp", e)


def build(debug=None):
    debug = debug or {}
    stop_after = debug.get("stop_after")
    nc = bass.Bass("TRN2", target_bir_lowering=False)

    def din(name, shape):
        return nc.dram_tensor(name, list(shape), F32, kind="ExternalInput").ap()

    x = din("x", [SEQ, D])
    meta = din("meta", [NMETA, D])
    norm1_w = din("norm1_w", [1, D])
    w_in = din("w_in", [D, 4624])
    cw_s = din("cw_s", [128, 12, 4])
    cb_s_col = din("cb_s_col", [128, 12])
    cb_s_row = din("cb_s_row", [1, 1536])
    dt_bias = din("dt_bias", [1, 16])
    a_log = din("a_log", [1, 16])
    d_skip = din("d_skip", [1, 16])
    ssd_nw = din("ssd_nw", [1, D])
    lvec = din("lvec", [128, 8, 9])
    lru_wa = din("lru_wa", [16, 64, 64])
    lru_wx = din("lru_wx", [16, 64, 64])
    w_out = din("w_out", [2 * D, D])
    norm2_w = din("norm2_w", [1, D])
    w_gate = din("w_gate", [D, DFF])
    w_up = din("w_up", [D, DFF])
    w_down = din("w_down", [DFF, D])
    final_nw = din("final_nw", [1, D])
    out = nc.dram_tensor("out", [SEQ, D], F32, kind="ExternalOutput").ap()
    dbg = {}
    for name, shape in debug.get("dumps", {}).items():
        dbg[name] = nc.dram_tensor("dbg_" + name, list(shape), F32, kind="ExternalOutput").ap()

    S = Sched()

    def OP(eng, meth, reads=(), writes=(), banks=(), **kw):
        S.op(eng, lambda e: getattr(e, meth)(**kw), reads, writes, banks)

    def DMA(queue, sem, out, in_, reads=(), writes=()):
        S.dma(queue, lambda e: e.dma_start(out=out, in_=in_), sem, reads, writes)

    def MM(out, lhsT, rhs, start, stop, reads, bank):
        S.op("pe", lambda e: e.matmul(out, lhsT=lhsT, rhs=rhs, start=start, stop=stop), reads, [("P", bank)], [bank])

    with ExitStack() as st:
        def sb(name, shape, dt):
            return st.enter_context(nc.sbuf_tensor(name, list(shape), dt))

        ARENA_KB = 202
        arena = sb("arena", [128, ARENA_KB * 512], BF16)

        def carve(off_b, shape, dt):
            nel = int(np.prod(shape[1:]))
            nbytes = nel * (4 if dt == F32 else 2)
            assert off_b % 4 == 0 and off_b + nbytes <= ARENA_KB * 1024, (off_b, nbytes)
            v = arena[0:shape[0], off_b // 2: off_b // 2 + nbytes // 2]
            if dt != BF16:
                v = v.bitcast(dt)
            if len(shape) == 3:
                v = v.rearrange("p (a b) -> p a b", a=shape[1])
            elif len(shape) == 4:
                v = v.rearrange("p (a b c) -> p a b c", a=shape[1], b=shape[2])
            return v

        KB = 1024
        P = [st.enter_context(nc.psum_tensor("P%d" % i, [128, 512], F32)) for i in range(8)]

        ident_bf = sb("ident_bf", [128, 128], BF16)
        ones_bf = sb("ones_bf", [128, 128], BF16)
        ones_f = sb("ones_f", [128, 128], F32)
        zero_f = sb("zero_f", [128, 128], F32)
        tri_f = sb("tri_f", [128, 128], F32)
        mneg_bf = sb("mneg_bf", [128, 128], BF16)
        stat = sb("stat", [128, 8, 20], F32)
        rstd_l = sb("rstd_l", [128, 16], F32)
        junk = sb("junk", [128, 1024], BF16)

        def dbg_dump(name, ap_sb, reads=()):
            if name in dbg:
                DMA("sp", "dbg", dbg[name], ap_sb, reads=list(reads))

        def finish():
            items = [(sk, c) for sk, c in S.dma_cnt.items() if sk in ("d:dbg", "d:out0", "d:out1")]
            S.final_wait("sp", items)
            S.barrier()
            S.emit(nc, st)

        OP("pool", "memset", writes=["ones_bf"], ap=ones_bf[:], constant=1.0)
        OP("pool", "memset", writes=["ones_f"], ap=ones_f[:], constant=1.0)
        OP("pool", "memset", writes=["zero_f"], ap=zero_f[:], constant=0.0)
        OP("pool", "affine_select", reads=["ones_bf"], writes=["ident_bf"], out=ident_bf[:], in_=ones_bf[:], pattern=[[1, 128]],
           compare_op=ALU.is_equal, fill=0.0, base=0, channel_multiplier=-1)
        OP("pool", "affine_select", reads=["ones_f"], writes=["tri_f"], out=tri_f[:], in_=ones_f[:], pattern=[[1, 128]],
           compare_op=ALU.is_ge, fill=0.0, base=0, channel_multiplier=-1)
        OP("pool", "affine_select", reads=["zero_f"], writes=["mneg_bf"], out=mneg_bf[:], in_=zero_f[:], pattern=[[1, 128]],
           compare_op=ALU.is_ge, fill=-30000.0, base=0, channel_multiplier=-1)

        def ut_reads(t0, n):
            return [("uT", i) for i in range(17) if toff(i) < t0 + n and toff(i) + tn(i) > t0]

        def norm_stats(n, src, src_reads, statcol, dim=D):
            ss = stat[:n, 0, statcol:statcol + 1]
            sq = stat[:n, 1, statcol:statcol + 1]
            rs = stat[:n, 2, statcol:statcol + 1]
            OP("dve", "scalar_tensor_tensor", reads=src_reads, writes=["junk", ("st0", statcol)], out=junk[:n], in0=src, scalar=1.0, in1=src,
               op0=ALU.mult, op1=ALU.mult, accum_out=ss)
            OP("act", "activation", reads=[("st0", statcol)], writes=[("st1", statcol)], out=sq, in_=ss, func=AF.Ln, scale=1.0 / dim, bias=EPS)
            OP("act", "activation", reads=[("st1", statcol)], writes=[("st2", statcol)], out=rs, in_=sq, func=AF.Exp, scale=-0.5)

        def norm_scale(n, src, src_reads, nw_bc, nwkey, ub, ubkey, statcol):
            rs = stat[:n, 2, statcol:statcol + 1]
            OP("dve", "scalar_tensor_tensor", reads=list(src_reads) + [("st2", statcol), nwkey], writes=[ubkey], out=ub[:n], in0=src,
               scalar=rs, in1=nw_bc[:n], op0=ALU.mult, op1=ALU.mult)

        def norm_apply(n, src, src_reads, nw_bc, nwkey, ub, ubkey, bankid, dst, dst_writes, statcol):
            norm_scale(n, src, src_reads, nw_bc, nwkey, ub, ubkey, statcol)
            norm_tr(n, ub, ubkey, bankid, dst, dst_writes)

        def norm_tr(n, ub, ubkey, bankid, dst, dst_writes):
            Pb = P[bankid][:].bitcast(BF16)
            for kc in range(8):
                S.op("pe", lambda e, kc=kc: e.transpose(out=Pb[:, kc * 128: kc * 128 + n], in_=ub[:n, kc * 128:(kc + 1) * 128],
                                                        identity=ident_bf[:n, :n]),
                     [ubkey, "ident_bf"], [("P", bankid)], [bankid])
            src_ps = Pb.rearrange("p (k m) -> p k m", k=8)[:, :, 0:n]
            OP("act", "activation", reads=[("P", bankid)], writes=dst_writes, banks=[bankid], out=dst, in_=src_ps, func=AF.Copy)

        KBb = KB
        yT_ssd = carve(33 * KB, [128, 8, SEQ], BF16)
        B0 = 65 * KB
        zs = carve(B0, [128, 16, 512], BF16)
        pre = carve(B0 + 16 * KB, [128, 6, T + 3], BF16)
        BT = carve(B0 + 41 * KB, [128, T], BF16)
        CT = carve(B0 + 41 * KB + 4352, [128, T], BF16)
        WS0 = B0 + 50 * KB
        wsl = [carve(WS0 + i * 8 * KB, [128, 8, 512], BF16) for i in range(3)]
        xs_all = carve(WS0, [128, 17, 512], BF16)
        btm_all = carve(WS0 + 17408, [128, 17, 128], BF16)
        CB = WS0 + 24 * KB
        diag_s = carve(CB, [128, 12, 4, 128], BF16)
        DI = carve(CB + 12 * KB, [128, 16, 128], BF16)
        sel2 = carve(CB + 16 * KB, [80, 16, 128], BF16)
        nws_bc = carve(CB + 20 * KB, [128, 1024], F32)
        cbrow_bf = carve(CB + 24 * KB, [1, 1536], BF16)
        csT2 = carve(CB + 27 * KB, [80, 17, 128], BF16)
        ncsT2 = carve(CB + 27 * KB + 4608, [80, 17, 128], BF16)
        SM = CB + 36 * KB
        names = ["xb", "ax", "ex", "dt", "dta", "cs", "ecs", "cd", "dec", "wdec", "tmp"]
        sm = {nm: carve(SM + k * 1152, [128, 17, 16], F32) for k, nm in enumerate(names)}
        SM2 = SM + 11 * 1152
        wdt = carve(SM2, [128, 8, 16], BF16)
        cws = carve(SM2 + 256, [128, 12, 4], F32)
        cbcol = carve(SM2 + 448, [128, 12], F32)
        dtb_bc = carve(SM2 + 512, [128, 16], F32)
        alog_bc = carve(SM2 + 576, [128, 16], F32)
        a_bc = carve(SM2 + 640, [128, 16], F32)
        d_bc = carve(SM2 + 704, [128, 16], F32)
        TQ = SM2 + 1 * KB
        dta3 = carve(TQ, [128, 17, 80], F32)
        onesX = carve(TQ + 5632, [80, 2048], BF16)
        hiA = carve(TQ + 5632 + 4 * KB, [80, 4, 128], BF16)
        midA = carve(TQ + 5632 + 5 * KB, [80, 4, 128], BF16)
        loA = carve(TQ + 5632 + 6 * KB, [80, 4, 128], BF16)
        r1 = carve(53 * KB, [80, 4, 128], F32)
        r2 = carve(55 * KB, [80, 4, 128], F32)

        wcnt = [0]

        def wload(src_ap, wslots, dst_cols=None):
            s = wcnt[0] % 3
            wcnt[0] += 1
            ncol = src_ap.shape[1]
            c0 = 0 if dst_cols is None else dst_cols
            DMA("pool", "w%d" % s, wslots[s][:, :, c0:c0 + ncol], src_ap.rearrange("(kc p) n -> p kc n", p=128), writes=[("wsl", s)])
            return s

        rr = [0]

        def nbank(nb=4):
            b = rr[0] % nb
            rr[0] += 1
            return b

        LW = B0 + 16 * KB
        xw_sb = carve(LW, [128, 512], BF16)
        xdt_sb = [carve(LW + (1 + i) * KB, [128, 512], BF16) for i in range(2)]
        prev32 = carve(LW + 3 * KB, [128, 512], F32)
        prev_bf = carve(LW + 5 * KB, [128, 512], BF16)
        es_sb = [carve(LW + 6 * KB + i * 2 * KB, [128, 512], F32) for i in range(4)]
        M_sb = [carve(LW + 14 * KB + i * KB, [128, 4, 128], BF16) for i in range(4)]
        t1_sb = carve(LW + 18 * KB, [128, 512], F32)
        gn_sb = carve(LW + 20 * KB, [128, 512], BF16)
        h8 = lambda ap: ap.rearrange("p (h d) -> p h d", h=8)

        def inproj_w(g):
            OP("pool", "memset", writes=["prepad"], ap=pre[:, :, 0:3], constant=0.0)
            if g == 1:
                s_x = 3
            else:
                s_x = wload(w_in[:, C_X + g * 512: C_X + (g + 1) * 512], wsl)
            s_bc = wload(w_in[:, C_B + g * 128: C_B + (g + 1) * 128], wsl)
            DMA("pool", "w%db" % s_bc, wsl[s_bc][:, :, 128:256], w_in[:, C_C + g * 128: C_C + (g + 1) * 128].rearrange("(kc p) n -> p kc n", p=128),
                writes=[("wsl2", s_bc)])
            s_z = wload(w_in[:, C_Z + g * 512: C_Z + (g + 1) * 512], wsl)
            return s_z, s_x, s_bc

        def inproj_x_unit(slots, ct, o, n):
            s_z, s_x, s_bc = slots
            s_w, c0 = (s_x, ct * 128) if ct < 4 else (s_bc, (ct - 4) * 128)
            bk = 2 + nbank()
            for kc in range(8):
                MM(P[bk][:, 0:n], wsl[s_w][:, kc, c0:c0 + 128], uT[:, kc, o:o + n], kc == 0, kc == 7, [("wsl", s_w), ("wsl2", s_w)] + ut_reads(o, n), bk)
            OP("dve", "tensor_copy", reads=[("P", bk)], writes=[("pre", ct, o)], banks=[bk], out=pre[:, ct, 3 + o: 3 + o + n], in_=P[bk][:, 0:n])

        def inproj_z(slots):
            s = slots[0]
            for i in range(1, 17):
                o = toff(i)
                bk = 2 + nbank()
                for kc in range(8):
                    MM(P[bk][:, :], uT[:, kc, o:o + 128], wsl[s][:, kc, :], kc == 0, kc == 7, [("wsl", s), ("uT", i)], bk)
                OP("act", "activation", reads=[("P", bk)], writes=[("zs", i)], banks=[bk], out=zs[:, i - 1, :], in_=P[bk][:, :], func=AF.Silu)

        def inproj(g):
            slots = inproj_w(g)
            for ct in range(6):
                for (o, n) in GRP:
                    inproj_x_unit(slots, ct, o, n)
            inproj_z(slots)

        uT = carve(0, [128, 8, T], BF16)
        A0 = 33 * KB
        xts = [carve(A0 + i * 4 * KB, [128, 1024], F32) for i in range(3)]
        ubs = [carve(A0 + 12 * KB + i * 2 * KB, [128, 1024], BF16) for i in range(2)]
        nw1_bc = carve(A0 + 16 * KB, [128, 1024], F32)
        DMA("sp", "k_nw1", nw1_bc, norm1_w[0:1, :].partition_broadcast(128), writes=["nwbc"])

        def a_load(i):
            srcd = meta[:, :] if i == 0 else x[(i - 1) * 128: i * 128, :]
            DMA("sp", "x%d" % (i % 3), xts[i % 3][:tn(i)], srcd, writes=[("xt", i % 3)])

        slots0 = inproj_w(0)
        pend = []
        a_load(0)
        a_load(1)
        norm_stats(tn(0), xts[0][:tn(0)], [("xt", 0)], 0)
        for i in range(17):
            if i + 2 < 17:
                a_load(i + 2)
            if i + 1 < 17:
                j = i + 1
                norm_stats(tn(j), xts[j % 3][:tn(j)], [("xt", j % 3)], j)
            n = tn(i)
            norm_scale(n, xts[i % 3][:n], [("xt", i % 3)], nw1_bc, "nwbc", ubs[i % 2], ("ub", i % 2), i)
            for _ in range(2):
                if pend:
                    inproj_x_unit(slots0, *pend.pop(0))
            norm_tr(n, ubs[i % 2], ("ub", i % 2), i % 2, uT[:, :, toff(i): toff(i) + n], [("uT", i)])
            if i == 0 or i % 4 == 0:
                o_, n_ = GRP[i // 4]
                pend += [(ct, o_, n_) for ct in range(6)]
        while pend:
            inproj_x_unit(slots0, *pend.pop(0))
        inproj_z(slots0)
        if "uT" in dbg:
            tmpf = carve(140 * KB, [128, T], F32)
            OP("dve", "tensor_copy", reads=[("uT", i) for i in range(17)], writes=["tmpf"], out=tmpf, in_=uT[:, 0, :])
            dbg_dump("uT", tmpf, ["tmpf"])
        if stop_after == "A":
            finish()
            return nc

        DMA("sp", "k0", cws, cw_s[:, :, :], writes=["cws"])
        DMA("sp", "k1", cbcol, cb_s_col[:, :], writes=["cbcol"])
        DMA("sp", "k2", dtb_bc, dt_bias[0:1, :].partition_broadcast(128), writes=["dtb"])
        DMA("sp", "k3", alog_bc, a_log[0:1, :].partition_broadcast(128), writes=["alog"])
        DMA("sp", "k4", d_bc, d_skip[0:1, :].partition_broadcast(128), writes=["dbc"])
        DMA("sp", "k5", nws_bc, ssd_nw[0:1, :].partition_broadcast(128), writes=["nwsbc"])
        DMA("pool", "k6", wdt, w_in[:, C_DT:C_DT + 16].rearrange("(kc p) n -> p kc n", p=128), writes=["wdt"])
        DMA("pool", "k7", cbrow_bf, cb_s_row[0:1, :], writes=["cbrow_bf"])
        OP("pool", "memset", writes=["onesX"], ap=onesX, constant=1.0)
        OP("pool", "memset", writes=["sel2"], ap=sel2, constant=0.0)
        OP("pool", "memset", writes=["csT2"], ap=csT2, constant=0.0)
        OP("pool", "memset", writes=["dta3"], ap=dta3, constant=0.0)
        for blk in (0, 32, 64):
            OP("pool", "affine_select", reads=["onesX", "sel2"], writes=["sel2"], out=sel2[blk:blk + 16],
               in_=onesX[blk:blk + 16].rearrange("p (a b) -> p a b", a=16), pattern=[[-1, 16], [0, 128]],
               compare_op=ALU.is_equal, fill=0.0, base=0, channel_multiplier=1)
        for cc in range(12):
            for k in range(4):
                OP("dve", "tensor_scalar", reads=["ident_bf", "cws"], writes=[("diag_s", cc)], out=diag_s[:, cc, k, :], in0=ident_bf[:],
                   scalar1=cws[:, cc, k:k + 1], scalar2=None, op0=ALU.mult)
        for h in range(16):
            OP("dve", "tensor_scalar", reads=["ident_bf", "dbc"], writes=["DI"], out=DI[:, h, :], in0=ident_bf[:], scalar1=d_bc[:, h:h + 1],
               scalar2=None, op0=ALU.mult)
        OP("act", "activation", reads=["alog"], writes=["a_bc0"], out=a_bc, in_=alog_bc, func=AF.Exp)
        OP("dve", "tensor_scalar", reads=["a_bc0"], writes=["a_bc"], out=a_bc, in0=a_bc, scalar1=-1.0, scalar2=None, op0=ALU.mult)

        PDT, PCS, PTOT, PCT = 2, 3, 4, 5
        for i in range(17):
            o = toff(i)
            for kc in range(8):
                MM(P[PDT][:, i * 16:(i + 1) * 16], uT[:, kc, o:o + 128], wdt[:, kc, :], kc == 0, kc == 7, ["wdt"] + ut_reads(o, 128), PDT)
        v3 = lambda ap: ap.rearrange("p (a b) -> p a b", a=17)
        pdt3 = v3(P[PDT][:, 0:272])
        bc17 = lambda ap: ap.unsqueeze(1).to_broadcast([128, 17, 16])
        OP("dve", "tensor_tensor", reads=[("P", PDT), "dtb"], writes=["xb"], banks=[PDT], out=sm["xb"], in0=pdt3, in1=bc17(dtb_bc), op=ALU.add)
        OP("dve", "scalar_tensor_tensor", reads=["xb"], writes=["ax"], out=sm["ax"], in0=sm["xb"], scalar=-1.0, in1=sm["xb"], op0=ALU.mult, op1=ALU.max)
        OP("act", "activation", reads=["ax"], writes=["ex"], out=sm["ex"], in_=sm["ax"], func=AF.Exp, scale=-1.0)
        OP("act", "activation", reads=["ex"], writes=["ex"], out=sm["ex"], in_=sm["ex"], func=AF.Ln, bias=1.0)
        OP("dve", "scalar_tensor_tensor", reads=["xb", "ex"], writes=["dt"], out=sm["dt"], in0=sm["xb"], scalar=0.0, in1=sm["ex"],
           op0=ALU.max, op1=ALU.add)
        OP("dve", "tensor_tensor", reads=["dt", "a_bc"], writes=["dta"], out=sm["dta"], in0=sm["dt"], in1=bc17(a_bc), op=ALU.mult)
        for blk in (0, 32, 64):
            OP("dve", "tensor_copy", reads=["dta", "dta3"], writes=["dta3"], out=dta3[:, :, blk:blk + 16], in_=sm["dta"])
        for c in range(17):
            MM(P[PCS][:, c * 16:(c + 1) * 16], tri_f[:, :], sm["dta"][:, c, :], True, True, ["tri_f", "dta"], PCS)
            MM(P[PTOT][:, c * 16:(c + 1) * 16], ones_f[:tn(c), :], sm["dta"][:tn(c), c, :], True, True, ["ones_f", "dta"], PTOT)
        pcs3, ptot3 = v3(P[PCS][:, 0:272]), v3(P[PTOT][:, 0:272])
        OP("dve", "tensor_copy", reads=[("P", PCS)], writes=["cs"], banks=[PCS], out=sm["cs"], in_=pcs3)
        OP("act", "activation", reads=[("P", PCS)], writes=["ecs"], banks=[PCS], out=sm["ecs"], in_=pcs3, func=AF.Exp)
        OP("act", "activation", reads=[("P", PTOT)], writes=["cd"], banks=[PTOT], out=sm["cd"], in_=ptot3, func=AF.Exp)
        OP("dve", "tensor_tensor", reads=[("P", PTOT), "cs"], writes=["tmp"], banks=[PTOT], out=sm["tmp"], in0=ptot3, in1=sm["cs"], op=ALU.subtract)
        OP("dve", "tensor_scalar", reads=["tmp"], writes=["tmp"], out=sm["tmp"], in0=sm["tmp"], scalar1=0.0, scalar2=None, op0=ALU.min)
        OP("act", "activation", reads=["tmp"], writes=["dec"], out=sm["dec"], in_=sm["tmp"], func=AF.Exp)
        OP("dve", "tensor_tensor", reads=["dec", "dt"], writes=["wdec"], out=sm["wdec"], in0=sm["dec"], in1=sm["dt"], op=ALU.mult)
        for c0 in range(0, 17, 4):
            cl = list(range(c0, min(17, c0 + 4)))
            nn = len(cl)
            for j, c in enumerate(cl):
                MM(P[PCT][:80, j * 128:(j + 1) * 128], dta3[:, c, :], tri_f[:, :], True, True, ["dta3", "tri_f"], PCT)
            srcp = P[PCT][:80, 0:nn * 128].rearrange("p (a b) -> p a b", a=nn)
            OP("act", "activation", reads=[("P", PCT)], writes=["hiA"], banks=[PCT], out=hiA[:, 0:nn, :], in_=srcp, func=AF.Copy)
            OP("dve", "tensor_tensor", reads=[("P", PCT), "hiA"], writes=["r1"], banks=[PCT], out=r1[:, 0:nn, :], in0=srcp, in1=hiA[:, 0:nn, :],
               op=ALU.subtract)
            OP("act", "activation", reads=["r1"], writes=["midA"], out=midA[:, 0:nn, :], in_=r1[:, 0:nn, :], func=AF.Copy)
            OP("dve", "tensor_tensor", reads=["r1", "midA"], writes=["r2"], out=r2[:, 0:nn, :], in0=r1[:, 0:nn, :], in1=midA[:, 0:nn, :], op=ALU.subtract)
            OP("act", "activation", reads=["r2"], writes=["loA"], out=loA[:, 0:nn, :], in_=r2[:, 0:nn, :], func=AF.Copy)
            OP("dve", "tensor_copy", reads=["hiA", "csT2"], writes=["csT2"], out=csT2[0:16, c0:c0 + nn, :], in_=hiA[0:16, 0:nn, :])
            OP("dve", "tensor_copy", reads=["midA", "csT2"], writes=["csT2"], out=csT2[32:48, c0:c0 + nn, :], in_=midA[32:48, 0:nn, :])
            OP("dve", "tensor_copy", reads=["loA", "csT2"], writes=["csT2"], out=csT2[64:80, c0:c0 + nn, :], in_=loA[64:80, 0:nn, :])
        OP("dve", "tensor_scalar", reads=["csT2"], writes=["ncsT2"], out=ncsT2, in0=csT2, scalar1=-1.0, scalar2=None, op0=ALU.mult)
        dbg_dump("dt", sm["dt"].rearrange("p a b -> p (a b)"), ["dt"])
        dbg_dump("cs", sm["cs"].rearrange("p a b -> p (a b)"), ["cs"])

        wsl.append(carve(TQ, [128, 8, 512], BF16))
        for g in range(2):
            h0 = g * 8
            if g == 1:
                S.barrier()
                inproj(1)
            S.barrier()
            for ct, dst, nm in ((4, BT, "BT"), (5, CT, "CT")):
                ccg = (8 + g) if ct == 4 else (10 + g)
                for (o, n) in GRP:
                    bk = nbank()
                    for k in range(4):
                        MM(P[bk][:, 0:n], diag_s[:, ccg, k, :], pre[:, ct, o + k: o + k + n], k == 0, k == 3, [], bk)
                    OP("act", "activation", reads=[("P", bk)], writes=[(nm, o)], banks=[bk], out=dst[:, o:o + n], in_=P[bk][:, 0:n],
                       func=AF.Silu, bias=cbcol[:, ccg:ccg + 1])
            for c in range(17):
                n, o = tn(c), toff(c)
                bk = nbank()
                for j in range(4):
                    ccg = g * 4 + j
                    for k in range(4):
                        MM(P[bk][:n, j * 128:(j + 1) * 128], pre[:, j, o + k: o + k + n], diag_s[:, ccg, k, :], k == 0, False, [], bk)
                    MM(P[bk][:n, j * 128:(j + 1) * 128], ones_bf[0:1, 0:n], cbrow_bf[0:1, ccg * 128:(ccg + 1) * 128], False, True, [], bk)
                OP("act", "activation", reads=[("P", bk)], writes=[("xs", c)], banks=[bk], out=xs_all[:n, c, :], in_=P[bk][:n, :], func=AF.Silu)
                bk = nbank()
                ccg = 8 + g
                for k in range(4):
                    MM(P[bk][:n, 0:128], pre[:, 4, o + k: o + k + n], diag_s[:, ccg, k, :], k == 0, False, [], bk)
                MM(P[bk][:n, 0:128], ones_bf[0:1, 0:n], cbrow_bf[0:1, ccg * 128:(ccg + 1) * 128], False, True, [], bk)
                OP("act", "activation", reads=[("P", bk)], writes=[("btm", c)], banks=[bk], out=btm_all[:n, c, :], in_=P[bk][:n, 0:128], func=AF.Silu)
            S.barrier()
            if g == 0 and "BT" in dbg:
                tmpf3 = carve(LW + 9 * KB, [128, T], F32)
                OP("dve", "tensor_copy", writes=["tmpf3"], out=tmpf3, in_=BT[:, :])
                dbg_dump("BT", tmpf3, ["tmpf3"])
                S.barrier()
            if g == 0:
                DMA("pool", "w3", wsl[3][:, :, :], w_in[:, C_X + 512: C_X + 1024].rearrange("(kc p) n -> p kc n", p=128), writes=[("wsl", 3)])
            PB, PS_, PY, PT_ = 0, 1, 6, 7
            PO = (2, 3)
            PSG = (4, 5)

            def x_cbt(c):
                if c >= 1:
                    o = toff(c)
                    MM(P[PB][:, (c % 2) * 128:(c % 2 + 1) * 128], BT[:, o:o + 128], CT[:, o:o + 128], True, True, [], PB)

            def x_mm(c, with_cbt=True):
                n, o = tn(c), toff(c)
                if with_cbt:
                    x_cbt(c)
                if c < 16:
                    OP("pool", "tensor_tensor", reads=["wdec"], writes=["xw"], out=h8(xw_sb[:n]), in0=h8(xs_all[:n, c, :]),
                       in1=sm["wdec"][:n, c, h0:h0 + 8].unsqueeze(2).to_broadcast([n, 8, 64]), op=ALU.mult)
                    MM(P[PS_][:, :], btm_all[:n, c, :], xw_sb[:n, :], True, True, ["xw"], PS_)
                if c >= 1:
                    MM(P[PO[c % 2]][:, :], CT[:, o:o + 128], prev_bf[:, :], True, True, ["prev_bf"], PO[c % 2])
                    OP("pool", "tensor_tensor", reads=["dt"], writes=[("xdt", c % 2)], out=h8(xdt_sb[c % 2][:n]), in0=h8(xs_all[:n, c, :]),
                       in1=sm["dt"][:n, c, h0:h0 + 8].unsqueeze(2).to_broadcast([n, 8, 64]), op=ALU.mult)

            def prev_chain(c):
                if c >= 16:
                    return
                if c == 0:
                    OP("dve", "tensor_copy", reads=[("P", PS_)], writes=["prev32"], banks=[PS_], out=prev32, in_=P[PS_][:, :])
                else:
                    OP("pool", "tensor_tensor", reads=["prev32", "cd"], writes=["prev32"], out=h8(prev32), in0=h8(prev32),
                       in1=sm["cd"][:, c, h0:h0 + 8].unsqueeze(2).to_broadcast([128, 8, 64]), op=ALU.mult)
                    OP("dve", "tensor_tensor", reads=["prev32", ("P", PS_)], writes=["prev32"], banks=[PS_], out=prev32, in0=prev32,
                       in1=P[PS_][:, :], op=ALU.add)
                OP("act", "activation", reads=["prev32"], writes=["prev_bf"], out=prev_bf, in_=prev32, func=AF.Copy)

            def look_seg(c):
                for hb in range(2):
                    bk = PSG[hb]
                    for hh in range(4):
                        h = h0 + hb * 4 + hh
                        osl = P[bk][:, hh * 128:(hh + 1) * 128]
                        MM(osl, sel2[:, h, :], csT2[:, c, :], True, False, ["sel2", "csT2"], bk)
                        MM(osl, ncsT2[:, c, :], sel2[:, h, :], False, False, ["sel2", "ncsT2"], bk)
                        MM(osl, ident_bf[:, :], mneg_bf[:, :], False, True, ["ident_bf", "mneg_bf"], bk)
                    e = es_sb[(c % 2) * 2 + hb]
                    OP("act", "activation", reads=[("P", bk)], writes=[("es", c % 2, hb)], banks=[bk], out=e, in_=P[bk][:, :], func=AF.Exp)

            def look_M(c):
                for hb in range(2):
                    OP("dve", "tensor_tensor", reads=[("es", c % 2, hb), ("P", PB)], writes=[("M", c % 2, hb)], banks=[PB], out=M_sb[(c % 2) * 2 + hb],
                       in0=es_sb[(c % 2) * 2 + hb].rearrange("p (a b) -> p a b", a=4),
                       in1=P[PB][:, (c % 2) * 128:(c % 2 + 1) * 128].unsqueeze(1).to_broadcast([128, 4, 128]), op=ALU.mult)

            def y_diag(c):
                for hb in range(2):
                    for hh in range(4):
                        hl = hb * 4 + hh
                        MM(P[PY][:, hl * 64:(hl + 1) * 64], M_sb[(c % 2) * 2 + hb][:, hh, :], xdt_sb[c % 2][:, hl * 64:(hl + 1) * 64], True, False,
                           [("M", c % 2, hb), ("xdt", c % 2)], PY)
                        MM(P[PY][:, hl * 64:(hl + 1) * 64], DI[:, h0 + hl, :], xs_all[:, c, hl * 64:(hl + 1) * 64], False, True, ["DI"], PY)

            def z1a(c):
                OP("dve", "tensor_tensor", reads=[("P", PO[c % 2]), "ecs"], writes=["t1"], banks=[PO[c % 2]], out=h8(t1_sb),
                   in0=h8(P[PO[c % 2]][:, :]), in1=sm["ecs"][:, c, h0:h0 + 8].unsqueeze(2).to_broadcast([128, 8, 64]), op=ALU.mult)
                OP("dve", "tensor_tensor", reads=["t1", ("P", PY)], writes=["t1"], banks=[PY], out=t1_sb, in0=t1_sb, in1=P[PY][:, :], op=ALU.add)

            def z1b(c):
                OP("pool", "tensor_tensor", reads=["t1", ("zs", c)], writes=["t1"], out=t1_sb, in0=t1_sb, in1=zs[:, c - 1, :], op=ALU.mult)
                ss, lnv, rs = stat[:, 3, c:c + 1], stat[:, 4, c:c + 1], stat[:, 5, c:c + 1]
                OP("act", "activation", reads=["t1"], writes=["junk", ("sg0", c)], out=junk[:, 0:512], in_=t1_sb, func=AF.Square, accum_out=ss)
                OP("act", "activation", reads=[("sg0", c)], writes=[("sg1", c)], out=lnv, in_=ss, func=AF.Ln, scale=1.0 / 512, bias=EPS)
                OP("act", "activation", reads=[("sg1", c)], writes=[("sg2", c)], out=rs, in_=lnv, func=AF.Exp, scale=-0.5)

            def z2(c):
                rs = stat[:, 5, c:c + 1]
                OP("dve", "scalar_tensor_tensor", reads=["t1", ("sg2", c), "nwsbc"], writes=["gn"], out=gn_sb, in0=t1_sb, scalar=rs,
                   in1=nws_bc[:, g * 512:(g + 1) * 512], op0=ALU.mult, op1=ALU.mult)
                Ptb = P[PT_][:].bitcast(BF16)
                for j in range(4):
                    S.op("pe", lambda e, j=j, Ptb=Ptb: e.transpose(out=Ptb[:, j * 128:(j + 1) * 128], in_=gn_sb[:, j * 128:(j + 1) * 128],
                                                                   identity=ident_bf[:, :]), ["gn", "ident_bf"], [("P", PT_)], [PT_])
                OP("act", "activation", reads=[("P", PT_)], writes=[("yT_ssd", g, c)], banks=[PT_],
                   out=yT_ssd[:, g * 4:(g + 1) * 4, (c - 1) * 128: c * 128], in_=Ptb[:, 0:512].rearrange("p (a b) -> p a b", a=4), func=AF.Copy)

            x_mm(0)
            prev_chain(0)
            x_mm(1)
            look_seg(1)
            look_M(1)
            prev_chain(1)
            for k in range(1, 17):
                if k >= 2:
                    z1a(k - 1)
                if k + 1 <= 16:
                    x_cbt(k + 1)
                    look_seg(k + 1)
                    x_mm(k + 1, with_cbt=False)
                y_diag(k)
                if k >= 2:
                    z1b(k - 1)
                if k + 1 <= 16:
                    look_M(k + 1)
                    prev_chain(k + 1)
                if k >= 2:
                    z2(k - 1)
            z1a(16)
            z1b(16)
            z2(16)
        S.barrier()
        if "yT_ssd" in dbg:
            tmpf4 = carve(B0, [128, 8, SEQ], F32)
            OP("dve", "tensor_copy", writes=["tmpf4"], out=tmpf4, in_=yT_ssd)
            dbg_dump("yT_ssd", tmpf4.rearrange("p a b -> p (a b)"), ["tmpf4"])
            S.barrier()
        if stop_after == "B":
            finish()
            return nc

        gate_sb = carve(97 * KB, [128, 8, SEQ], BF16)
        xpre = carve(129 * KB, [128, 8, T + 3], BF16)
        WC0 = 162 * KB
        wsc = [carve(WC0 + i * 8 * KB, [128, 8, 512], BF16) for i in range(3)]
        CC = 186 * KB
        diag_l = carve(CC, [128, 8, 4, 128], BF16)
        bd_a = carve(CC + 8 * KB, [128, 8, 128], BF16)
        bd_x = carve(CC + 10 * KB, [128, 8, 128], BF16)
        lv = carve(CC + 12 * KB, [128, 8, 9], F32)
        lvh = carve(CC + 12 * KB + 512, [128, 8, 9], F32)
        hc8 = carve(CC + 12 * KB + 832, [128, 8], F32)
        c8 = carve(CC + 12 * KB + 288, [128, 8], F32)
        hlast = carve(CC + 12 * KB + 320, [128, 8], F32)
        lt = [carve(CC + 12 * KB + 352 + 32 * k, [128, 8], F32) for k in range(3)]
        DMA("sp", "k_lv", lv, lvec[:, :, :], writes=["lv"])
        OP("pool", "memset", writes=["bd_a"], ap=bd_a, constant=0.0)
        OP("pool", "memset", writes=["bd_x"], ap=bd_x, constant=0.0)
        OP("pool", "memset", writes=["xprepad"], ap=xpre[:, :, 0:3], constant=0.0)
        for wsrc, bd, nm in ((lru_wa, bd_a, "bd_a"), (lru_wx, bd_x, "bd_x")):
            wv = wsrc.rearrange("(c two) i j -> two i c j", two=2)
            for par in range(2):
                DMA("pool", "kb_%s%d" % (nm, par), bd[par * 64:(par + 1) * 64, :, par * 64:(par + 1) * 64], wv[par], reads=[nm],
                    writes=[nm + "w%d" % par])
        BD = ["bd_a", "bd_x", "bd_aw0", "bd_aw1", "bd_xw0", "bd_xw1"]
        for cc in range(8):
            for k in range(4):
                OP("dve", "tensor_scalar", reads=["ident_bf", "lv"], writes=[("diag_l", cc)], out=diag_l[:, cc, k, :], in0=ident_bf[:],
                   scalar1=lv[:, cc, k:k + 1], scalar2=None, op0=ALU.mult)
        OP("dve", "tensor_scalar", reads=["lv"], writes=["lvh"], out=lvh, in0=lv, scalar1=0.5, scalar2=None, op0=ALU.mult)
        lam = lv[:, :, 7]
        OP("dve", "scalar_tensor_tensor", reads=["lv"], writes=["lt0"], out=lt[0], in0=lam, scalar=-1.0, in1=lam, op0=ALU.mult, op1=ALU.max)
        OP("act", "activation", reads=["lt0"], writes=["lt0"], out=lt[0], in_=lt[0], func=AF.Exp, scale=-1.0)
        OP("act", "activation", reads=["lt0"], writes=["lt0"], out=lt[0], in_=lt[0], func=AF.Ln, bias=1.0)
        OP("dve", "tensor_scalar", reads=["lv"], writes=["lt1"], out=lt[1], in0=lam, scalar1=-1.0, scalar2=0.0, op0=ALU.mult, op1=ALU.max)
        OP("dve", "tensor_tensor", reads=["lt0", "lt1"], writes=["lt2"], out=lt[2], in0=lt[0], in1=lt[1], op=ALU.add)
        OP("dve", "tensor_scalar", reads=["lt2"], writes=["c8"], out=c8, in0=lt[2], scalar1=-8.0, scalar2=None, op0=ALU.mult)
        OP("dve", "tensor_scalar", reads=["lt2"], writes=["hc8"], out=hc8, in0=lt[2], scalar1=-4.0, scalar2=None, op0=ALU.mult)
        for part in range(4):
            base = (C_LG if part < 2 else C_LX) + (part % 2) * 512
            s = wload(w_in[:, base: base + 512], wsc)
            for ctl in range(4):
                cc = (part % 2) * 4 + ctl
                for gi, (o, n) in enumerate(GRP):
                    if part < 2 and gi == 0:
                        continue
                    bk = nbank()
                    for kc in range(8):
                        MM(P[bk][:, 0:n], wsc[s][:, kc, ctl * 128:(ctl + 1) * 128], uT[:, kc, o:o + n], kc == 0, kc == 7, [("wsl", s)], bk)
                    if part < 2:
                        OP("act", "activation", reads=[("P", bk)], writes=[("gate", cc, gi)], banks=[bk], out=gate_sb[:, cc, o - 16: o - 16 + n],
                           in_=P[bk][:, 0:n], func=AF.Gelu_apprx_tanh)
                    else:
                        OP("dve", "tensor_copy", reads=[("P", bk)], writes=[("xpre", cc, gi)], banks=[bk], out=xpre[:, cc, 3 + o: 3 + o + n],
                           in_=P[bk][:, 0:n])
        S.barrier()
        yT_lru = gate_sb
        rA = [carve(i * 2 * KB, [128, 512], F32) for i in range(8)]
        iB = [carve(16 * KB + i * 2 * KB, [128, 512], F32) for i in range(8)]
        qq = [carve(65 * KB + i * 2 * KB, [128, 512], F32) for i in range(8)]
        xr32 = [carve(81 * KB + i * 2 * KB, [128, 512], F32) for i in range(8)]
        bd_af = carve(162 * KB, [128, 8, 128], F32)
        bd_xf = carve(166 * KB, [128, 8, 128], F32)
        OP("pool", "memset", writes=["bd_af"], ap=bd_af, constant=0.0)
        OP("pool", "memset", writes=["bd_xf"], ap=bd_xf, constant=0.0)
        for wsrc, bdf, nm in ((lru_wa, bd_af, "bd_af"), (lru_wx, bd_xf, "bd_xf")):
            wv = wsrc.rearrange("(c two) i j -> two i c j", two=2)
            for par in range(2):
                DMA("sp", "kf_%s%d" % (nm, par), bdf[par * 64:(par + 1) * 64, :, par * 64:(par + 1) * 64], wv[par], reads=[nm],
                    writes=[nm + "w%d" % par])
        BDF = ["bd_af", "bd_xf", "bd_afw0", "bd_afw1", "bd_xfw0", "bd_xfw1"]
        h_sb = [carve(170 * KB + i * 2 * KB, [128, 512], F32) for i in range(2)]
        ysq = [carve(174 * KB + i * KB, [128, 512], BF16) for i in range(4)]
        ssrow = carve(178 * KB, [1, 512], F32)
        PSS, PC = 6, 7
        ssteps = [(gi, o, n, q4) for gi, (o, n) in enumerate(GRP) for q4 in range(2)]
        sub = [0]

        def f_sub(s, j):
            gi, o, n, q4 = ssteps[s]
            cc, b = q4 * 4 + j, (s % 2) * 4 + j
            p = (s * 4 + j) % 2
            bx, br, bi = p, 2 + p, 4 + p
            for kk in range(4):
                MM(P[bx][:, 0:n], diag_l[:, cc, kk, :], xpre[:, cc, o + kk: o + kk + n], kk == 0, kk == 3, [], bx)
            OP("dve", "tensor_scalar", reads=[("P", bx)], writes=[("xr32", b)], banks=[bx], out=xr32[b][:, :n], in0=P[bx][:, 0:n],
               scalar1=lv[:, cc, 4:5], scalar2=None, op0=ALU.add)

        def f_gate(s, j):
            gi, o, n, q4 = ssteps[s]
            cc, b = q4 * 4 + j, (s % 2) * 4 + j
            p = (s * 4 + j) % 2
            br, bi = 2 + p, 4 + p
            MM(P[br][:, 0:n], bd_af[:, cc, :], xr32[b][:, :n], True, True, [("xr32", b)] + BDF, br)
            MM(P[bi][:, 0:n], bd_xf[:, cc, :], xr32[b][:, :n], True, True, [("xr32", b)] + BDF, bi)

        def f_act(s, j):
            gi, o, n, q4 = ssteps[s]
            cc, b = q4 * 4 + j, (s % 2) * 4 + j
            p = (s * 4 + j) % 2
            br, bi = 2 + p, 4 + p
            OP("act", "activation", reads=[("P", br)], writes=[("rA", b)], banks=[br], out=rA[b][:, :n], in_=P[br][:, 0:n], func=AF.Tanh,
               scale=0.5, bias=lvh[:, cc, 5:6])
            OP("act", "activation", reads=[("P", bi)], writes=[("iB", b)], banks=[bi], out=iB[b][:, :n], in_=P[bi][:, 0:n], func=AF.Tanh,
               scale=0.5, bias=lvh[:, cc, 6:7])
            OP("act", "activation", reads=[("rA", b)], writes=[("qq", b)], out=qq[b][:, :n], in_=rA[b][:, :n], func=AF.Exp,
               scale=c8[:, cc:cc + 1], bias=c8[:, cc:cc + 1])
            OP("act", "activation", reads=[("rA", b)], writes=[("rA", b)], out=rA[b][:, :n], in_=rA[b][:, :n], func=AF.Exp,
               scale=hc8[:, cc:cc + 1], bias=hc8[:, cc:cc + 1])
            OP("dve", "scalar_tensor_tensor", reads=[("iB", b), ("xr32", b)], writes=[("iB", b)], out=iB[b][:, :n], in0=iB[b][:, :n], scalar=1.0,
               in1=xr32[b][:, :n], op0=ALU.add, op1=ALU.mult)

        def s_mid(s):
            gi, o, n, q4 = ssteps[s]
            bs = [(s % 2) * 4 + j for j in range(4)]
            for b in bs:
                OP("act", "activation", reads=[("qq", b)], writes=[("qq", b)], out=qq[b][:, :n], in_=qq[b][:, :n], func=AF.Sqrt, scale=-0.25, bias=0.25)
            for b in bs:
                OP("pool", "tensor_tensor", reads=[("iB", b), ("qq", b)], writes=[("iB", b)], out=iB[b][:, :n], in0=iB[b][:, :n], in1=qq[b][:, :n],
                   op=ALU.mult)

        def b_sub(s, j):
            gi, o, n, q4 = ssteps[s]
            cc, b = q4 * 4 + j, (s % 2) * 4 + j
            hb, hk = h_sb[j % 2], ("h", j % 2)
            init = 0.0 if gi == 0 else hlast[:, cc:cc + 1]
            OP("dve", "tensor_tensor_scan", reads=[("rA", b), ("iB", b), ("hlast", cc)], writes=[hk], out=hb[:, :n], data0=rA[b][:, :n],
               data1=iB[b][:, :n], initial=init, op0=ALU.mult, op1=ALU.add)
            if gi < 4:
                OP("dve", "tensor_copy", reads=[hk], writes=[("hlast", cc)], out=hlast[:, cc:cc + 1], in_=hb[:, n - 1:n])
            if gi >= 1:
                gsl = gate_sb[:, cc, o - 16: o - 16 + n]
                OP("dve", "tensor_tensor", reads=[hk, ("gate", cc, gi)], writes=[hk], out=hb[:, :n], in0=hb[:, :n], in1=gsl, op=ALU.mult)
                OP("pool", "tensor_tensor", reads=[hk], writes=[("ysq", j)], out=ysq[j][:, :n], in0=hb[:, :n], in1=hb[:, :n], op=ALU.mult)
                OP("pool", "tensor_scalar", reads=[hk, ("gate", cc, gi)], writes=[("gate", cc, gi)], out=gsl, in0=hb[:, :n],
                   scalar1=lv[:, cc, 8:9], scalar2=1.0, op0=ALU.mult, op1=ALU.mult)

        def s_ss(s):
            gi, o, n, q4 = ssteps[s]
            if gi == 0:
                return
            for j in range(4):
                cc = q4 * 4 + j
                MM(P[PSS][0:1, 0:n], ones_bf[:, 0:1], ysq[j][:, :n], cc == 0, cc == 7, [("ysq", j)], PSS)
            if q4 == 1:
                OP("dve", "tensor_copy", reads=[("P", PSS)], writes=["ssrow"], banks=[PSS], out=ssrow[0:1, :], in_=P[PSS][0:1, :])
                for j in range(4):
                    MM(P[PC][:, j:j + 1], ssrow[0:1, j * 128:(j + 1) * 128], ones_f[0:1, 0:1], True, True, ["ssrow"], PC)
                OP("act", "activation", reads=[("P", PC)], writes=["sl1"], banks=[PC], out=stat[:, 6, 0:4], in_=P[PC][:, 0:4], func=AF.Ln,
                   scale=1.0 / 1024, bias=EPS)
                OP("act", "activation", reads=["sl1"], writes=["rstd_l"], out=rstd_l[:, (gi - 1) * 4: gi * 4], in_=stat[:, 6, 0:4], func=AF.Exp,
                   scale=-0.5)

        NS = len(ssteps)
        f_sub(0, 0)
        for j in range(4):
            if j + 1 < 4:
                f_sub(0, j + 1)
            f_gate(0, j)
            f_act(0, j)
        s_mid(0)
        wo_sb = carve(65 * KB, [128, 16, 1024], BF16)

        def wo_prefetch(par):
            for base, nm in ((0, "qq"), (8, "xr32")):
                c0 = base + par * 4
                DMA("pool", "wo%d" % (par * 2 + (base // 8)), wo_sb[:, c0:c0 + 4, :],
                    w_out[c0 * 128:(c0 + 4) * 128, :].rearrange("(c p) n -> p c n", p=128),
                    writes=[(nm, par * 4 + j) for j in range(4)])

        for s in range(NS):
            if s == NS - 2:
                wo_prefetch(s % 2)
            if s == NS - 1:
                wo_prefetch(s % 2)
            for j in range(4):
                if s + 1 < NS:
                    f_sub(s + 1, j)
                    if j >= 1:
                        f_gate(s + 1, j - 1)
                if j == 0 and s >= 1:
                    s_ss(s - 1)
                b_sub(s, j)
                if s + 1 < NS and j >= 2:
                    f_act(s + 1, j - 2)
            if s + 1 < NS:
                f_gate(s + 1, 3)
                f_act(s + 1, 2)
                f_act(s + 1, 3)
                s_mid(s + 1)
        s_ss(NS - 1)
        S.barrier()
        if "yT_lru" in dbg:
            tmpf5 = carve(0, [128, 4, SEQ], F32)
            OP("dve", "tensor_copy", writes=["tmpf5"], out=tmpf5, in_=yT_lru[:, 0:4, :])
            dbg_dump("yT_lru", tmpf5.rearrange("p a b -> p (a b)"), ["tmpf5"])
            dbg_dump("rstd_l", rstd_l[:, :], [])
            S.barrier()
        if stop_after == "C":
            finish()
            return nc

        h1 = carve(129 * KB, [128, 16, 1024], F32)
        u2T = carve(0, [128, 8, SEQ], BF16)
        ub2 = [carve(193 * KB + i * 2 * KB, [128, 1024], BF16) for i in range(2)]
        nw2_bc = carve(197 * KB, [128, 1024], F32)
        DMA("sp", "k_nw2", nw2_bc, norm2_w[0:1, :].partition_broadcast(128), writes=["nw2bc"])

        def d_load(i):
            rd = [("h1", i - 3, 0)] if i >= 3 else []
            DMA("sp", "x%d" % (i % 3), h1[:, i, :], x[i * 128:(i + 1) * 128, :], reads=rd, writes=[("h1", i, 0), ("h1", i, 1)])

        d_load(0)
        d_load(1)
        slot = 0
        H1R = lambda t: [("h1", t, 0), ("h1", t, 1)]

        def n2_stats(t):
            norm_stats(128, h1[:, t, :], H1R(t), t)

        def n2_apply(t):
            norm_apply(128, h1[:, t, :], H1R(t), nw2_bc, "nw2bc", ub2[t % 2], ("ub2", t % 2), 6 + t % 2,
                       u2T[:, :, t * 128:(t + 1) * 128], [("u2T", t)], t)

        for i in range(16):
            if i + 2 < 16:
                d_load(i + 2)
            if i >= 1:
                n2_stats(i - 1)
            banks_i = []
            for dh in range(2):
                bs, bl = (slot % 3) * 2, (slot % 3) * 2 + 1
                slot += 1
                banks_i.append((bs, bl))
                for cc in range(8):
                    MM(P[bs][:, :], yT_ssd[:, cc, i * 128:(i + 1) * 128], wo_sb[:, cc, dh * 512:(dh + 1) * 512], cc == 0, cc == 7, [], bs)
                for cc in range(8):
                    MM(P[bl][:, :], yT_lru[:, cc, i * 128:(i + 1) * 128], wo_sb[:, 8 + cc, dh * 512:(dh + 1) * 512], cc == 0, cc == 7, [], bl)
            if i >= 2:
                n2_apply(i - 2)
            for dh in range(2):
                bs, bl = banks_i[dh]
                hsl = h1[:, i, dh * 512:(dh + 1) * 512]
                OP("dve", "tensor_tensor", reads=[("P", bs), ("h1", i, dh)], writes=[("h1", i, dh)], banks=[bs], out=hsl, in0=hsl, in1=P[bs][:, :],
                   op=ALU.add)
                OP("dve", "scalar_tensor_tensor", reads=[("P", bl), ("h1", i, dh)], writes=[("h1", i, dh)], banks=[bl], out=hsl, in0=P[bl][:, :],
                   scalar=rstd_l[:, i:i + 1], in1=hsl, op0=ALU.mult, op1=ALU.add)
        n2_stats(15)
        n2_apply(14)
        n2_apply(15)
        S.barrier()
        if "h1" in dbg:
            dbg_dump("h1", h1.rearrange("p a b -> p (a b)"), [])
            S.barrier()
        if stop_after == "D":
            finish()
            return nc

        nwf_bc = carve(121 * KB, [128, 1024], F32)
        DMA("sp", "k_nwf", nwf_bc, final_nw[0:1, :].partition_broadcast(128), writes=["nwfbc"])
        fslots = [33 * KB, 69 * KB]
        wgs = [carve(b, [128, 8, 768], BF16) for b in fslots]
        wus = [carve(b + 12 * KB, [128, 8, 768], BF16) for b in fslots]
        wds = [carve(b + 24 * KB, [128, 6, 1024], BF16) for b in fslots]
        hff = [carve(105 * KB + i * 6 * KB, [128, 6, 512], BF16) for i in range(2)]
        sgb = [carve(117 * KB + i * 2 * KB, [128, 512], F32) for i in range(2)]
        groups = [(0, 2), (2, 4), (6, 5), (11, 5), (16, 6)]
        NG = len(groups)

        def fload(q):
            f0, nf = groups[q]
            sl = q % 2
            DMA("pool", "fwg%d" % sl, wgs[sl][:, :, 0:nf * 128], w_gate[:, f0 * 128:(f0 + nf) * 128].rearrange("(kc p) n -> p kc n", p=128),
                writes=[("wg", sl)])
            DMA("pool", "fwu%d" % sl, wus[sl][:, :, 0:nf * 128], w_up[:, f0 * 128:(f0 + nf) * 128].rearrange("(kc p) n -> p kc n", p=128),
                writes=[("wu", sl)])
            DMA("pool", "fwd%d" % sl, wds[sl][:, 0:nf, :], w_down[f0 * 128:(f0 + nf) * 128, :].rearrange("(f p) n -> p f n", p=128),
                writes=[("wd", sl)])

        fload(0)
        fload(1)
        step = 0
        for q in range(NG):
            f0, nf = groups[q]
            sl = q % 2
            FW = [("wg", sl), ("wu", sl), ("wd", sl)]
            for tg in range(4):
                hb = hff[tg % 2]
                for f in range(nf):
                    bg, bu = step % 2, 2 + step % 2
                    step += 1
                    for kc in range(8):
                        MM(P[bg][:, :], wgs[sl][:, kc, f * 128:(f + 1) * 128], u2T[:, kc, tg * 512:(tg + 1) * 512], kc == 0, kc == 7, [("wg", sl)], bg)
                    for kc in range(8):
                        MM(P[bu][:, :], wus[sl][:, kc, f * 128:(f + 1) * 128], u2T[:, kc, tg * 512:(tg + 1) * 512], kc == 0, kc == 7, [("wu", sl)], bu)
                    sg = sgb[step % 2]
                    OP("act", "activation", reads=[("P", bg)], writes=[("sg", step % 2)], banks=[bg], out=sg, in_=P[bg][:, :], func=AF.Silu)
                    OP("dve", "tensor_tensor", reads=[("sg", step % 2), ("P", bu)], writes=[("hff", tg % 2, f)], banks=[bu], out=hb[:, f, :], in0=sg,
                       in1=P[bu][:, :], op=ALU.mult)
                for j in range(4):
                    ti = tg * 4 + j
                    for dh in range(2):
                        bd_ = 4 + (j * 2 + dh) % 4
                        for f in range(nf):
                            MM(P[bd_][:, :], hb[:, f, j * 128:(j + 1) * 128], wds[sl][:, f, dh * 512:(dh + 1) * 512], f == 0, f == nf - 1,
                               [("hff", tg % 2, f), ("wd", sl)], bd_)
                        hsl = h1[:, ti, dh * 512:(dh + 1) * 512]
                        OP("dve", "tensor_tensor", reads=[("P", bd_), ("h1", ti, dh)], writes=[("h1", ti, dh)], banks=[bd_], out=hsl, in0=hsl,
                           in1=P[bd_][:, :], op=ALU.add)
                    if q == NG - 1:
                        src = h1[:, ti, :]
                        rd = [("h1", ti, 0), ("h1", ti, 1)]
                        norm_stats(128, src, rd, ti)
                        OP("dve", "scalar_tensor_tensor", reads=rd + [("st2", ti), "nwfbc"], writes=rd, out=src, in0=src,
                           scalar=stat[:, 2, ti:ti + 1], in1=nwf_bc, op0=ALU.mult, op1=ALU.mult)
                        DMA("sp", "out%d" % (ti % 2), out[ti * 128:(ti + 1) * 128, :], src, reads=rd)
            if q + 2 < NG:
                fload(q + 2)
        finish()
    return nc


def host_inputs(inp):
    f = lambda a: np.ascontiguousarray(np.asarray(a, dtype=np.float32))
    cw = f(inp["ssd_conv_w"])[0]
    cw_s = f(cw.reshape(4, 12, 128).transpose(2, 1, 0))
    cb = f(inp["ssd_conv_b"])[0]
    cb_s_col = f(cb.reshape(12, 128).T)
    lw = f(inp["lru_conv_w"])[0]
    col = lambda v: f(v).reshape(8, 128).T
    lvec = np.stack([col(lw[0]), col(lw[1]), col(lw[2]), col(lw[3]), col(inp["lru_conv_b"]), col(inp["lru_ba"]), col(inp["lru_bx"]),
                     col(inp["lru_lambda"]), col(inp["lru_norm_w"])], axis=-1)
    shared = {
        "meta": f(inp["meta_tokens"]), "norm1_w": f(inp["norm1_w"]), "w_in": f(inp["w_in"])[0], "cw_s": cw_s, "cb_s_col": cb_s_col,
        "cb_s_row": f(cb.reshape(1, 1536)), "dt_bias": f(inp["ssd_dt_bias"]), "a_log": f(inp["ssd_a_log"]), "d_skip": f(inp["ssd_d"]),
        "ssd_nw": f(inp["ssd_norm_w"]), "lvec": f(lvec), "lru_wa": f(inp["lru_wa"])[0], "lru_wx": f(inp["lru_wx"])[0],
        "w_out": f(inp["w_out"])[0], "norm2_w": f(inp["norm2_w"]), "w_gate": f(inp["w_gate"])[0], "w_up": f(inp["w_up"])[0],
        "w_down": f(inp["w_down"])[0], "final_nw": f(inp["final_norm_w"]).reshape(1, D),
    }
    xs = f(inp["x"])
    return [dict(shared, x=xs[b]) for b in range(8)]


def kernel(**inputs):
    nc = build()
    in_maps = host_inputs(inputs)
    res = run_bass_kernel_spmd(nc, in_maps, core_ids=list(range(8)))
    return np.stack([np.asarray(r["out"], dtype=np.float32) for r in res.results], axis=0)
```
